# Optimizing a Trainium2 kernel written in Bass

```python
import math
import jax
import jax.numpy as jnp
from jax import lax
import numpy as np

D_MODEL = 1024
BATCH = 8
SEQ = 8192
DEPTH = 2

HEAD_DIM = 64
ATTN_SCALE = HEAD_DIM ** -0.5
SWA_HEADS = D_MODEL // 128
SWA_KV_HEADS = SWA_HEADS // 4
SWA_GROUP = SWA_HEADS // SWA_KV_HEADS
SWA_WINDOW = 128
SWA_BLOCK = 128
SC_WIDTH = D_MODEL // 2
SC_KSIZE = 3
MOBA_HEADS = D_MODEL // 128
MOBA_BLOCK = 256
MOBA_TOPK = 3
MOBA_QCHUNK = 32
N_ALIBI_HEADS = SWA_HEADS + MOBA_HEADS
FFN_HIDDEN = -(-8 * D_MODEL // (3 * 256)) * 256
ALPHA = (2 * DEPTH) ** 0.25
BETA = (8 * DEPTH) ** -0.25
LN_EPS = 1e-5

A_Q = SWA_HEADS * HEAD_DIM
A_KV = SWA_KV_HEADS * HEAD_DIM
C_QKV = MOBA_HEADS * HEAD_DIM
PROJ_SIZES = (A_Q, A_KV, A_KV,
              SC_WIDTH, SC_WIDTH, SC_WIDTH,
              C_QKV, C_QKV, C_QKV,
              D_MODEL, D_MODEL, D_MODEL)
PROJ_SPLITS = tuple(int(s) for s in np.cumsum(PROJ_SIZES)[:-1])
PROJ_WIDTH = int(sum(PROJ_SIZES))

kernel_name = 'hybrid_swa_shortconv_moba_deepnorm'


def alibi_slopes():
    i = jnp.arange(N_ALIBI_HEADS, dtype=jnp.float32)
    s = jnp.exp2(-8.0 * (i + 1.0) / N_ALIBI_HEADS)
    return s[:SWA_HEADS], s[SWA_HEADS:]


def layer_norm(x, g, b):
    xf = x.astype(jnp.float32)
    mu = jnp.mean(xf, axis=-1, keepdims=True)
    var = jnp.mean(jnp.square(xf - mu), axis=-1, keepdims=True)
    y = (xf - mu) * lax.rsqrt(var + LN_EPS) * g.astype(jnp.float32) + b.astype(jnp.float32)
    return y.astype(x.dtype)


def sliding_window_attention(q, k, v, sinks, slopes):
    B, S = q.shape[0], q.shape[1]
    nb = S // SWA_BLOCK
    qb = q.reshape(B, nb, SWA_BLOCK, SWA_KV_HEADS, SWA_GROUP, HEAD_DIM).astype(jnp.float32)
    kb = k.reshape(B, nb, SWA_BLOCK, SWA_KV_HEADS, HEAD_DIM)
    vb = v.reshape(B, nb, SWA_BLOCK, SWA_KV_HEADS, HEAD_DIM)
    shift = ((0, 0), (1, 0), (0, 0), (0, 0), (0, 0))
    kw = jnp.concatenate([jnp.pad(kb, shift)[:, :-1], kb], axis=2)
    vw = jnp.concatenate([jnp.pad(vb, shift)[:, :-1], vb], axis=2)
    logits = jnp.einsum('bnqhgd,bnkhd->bnhgqk', qb, kw.astype(jnp.float32)) * ATTN_SCALE
    blk = jnp.arange(nb)[:, None] * SWA_BLOCK
    qpos = blk + jnp.arange(SWA_BLOCK)[None, :]
    kpos = blk - SWA_BLOCK + jnp.arange(2 * SWA_BLOCK)[None, :]
    dist = qpos[:, :, None] - kpos[:, None, :]
    allowed = (dist >= 0) & (dist < SWA_WINDOW) & (kpos[:, None, :] >= 0)
    sl = slopes.reshape(SWA_KV_HEADS, SWA_GROUP)[:, :, None, None]
    logits = logits - sl * dist[:, None, None].astype(jnp.float32)
    logits = jnp.where(allowed[:, None, None], logits, -jnp.inf)
    sink = sinks.astype(jnp.float32).reshape(SWA_KV_HEADS, SWA_GROUP)[:, :, None, None]
    m = jnp.maximum(jnp.max(logits, axis=-1, keepdims=True), sink)
    p = jnp.exp(logits - m)
    denom = jnp.sum(p, axis=-1, keepdims=True) + jnp.exp(sink - m)
    out = jnp.einsum('bnhgqk,bnkhd->bnqhgd', p / denom, vw.astype(jnp.float32))
    return out.reshape(B, S, SWA_HEADS * HEAD_DIM).astype(q.dtype)


def gated_short_conv(h, gate_b, gate_c, conv_w):
    u = gate_c * h
    up = jnp.pad(u, ((0, 0), (SC_KSIZE - 1, 0), (0, 0)))
    S = h.shape[1]
    conv = up[:, 0:S] * conv_w[0] + up[:, 1:S + 1] * conv_w[1] + up[:, 2:S + 2] * conv_w[2]
    return gate_b * conv


def moba_attention(q, k, v, slopes):
    B, S = q.shape[0], q.shape[1]
    sp = -(-S // MOBA_BLOCK) * MOBA_BLOCK
    pad = ((0, 0), (0, sp - S), (0, 0), (0, 0))
    q, k, v = [jnp.pad(t, pad).transpose(0, 2, 1, 3) for t in (q, k, v)]
    nblk = sp // MOBA_BLOCK
    top = min(MOBA_TOPK, nblk)
    kbl = k.reshape(B, MOBA_HEADS, nblk, MOBA_BLOCK, HEAD_DIM)
    vbl = v.reshape(B, MOBA_HEADS, nblk, MOBA_BLOCK, HEAD_DIM)
    kmean = jnp.mean(kbl.astype(jnp.float32), axis=3)
    gate = jnp.einsum('bhsd,bhnd->bhsn', q.astype(jnp.float32), kmean)
    cur = jnp.arange(sp) // MOBA_BLOCK
    past = jnp.arange(nblk)[None, :] < cur[:, None]
    gate = jnp.where(past, gate, -jnp.inf)
    _, idx = lax.top_k(gate, top)
    nch = sp // MOBA_QCHUNK
    q_ch = q.reshape(B, MOBA_HEADS, nch, MOBA_QCHUNK, HEAD_DIM).transpose(2, 0, 1, 3, 4)
    idx_ch = idx.reshape(B, MOBA_HEADS, nch, MOBA_QCHUNK, top).transpose(2, 0, 1, 3, 4)
    bi = jnp.arange(B)[:, None, None, None]
    hi = jnp.arange(MOBA_HEADS)[None, :, None, None]
    sl4 = slopes[None, :, None, None]
    sl5 = slopes[None, :, None, None, None]
    koff = jnp.arange(MOBA_BLOCK)

    def chunk(args):
        c, qc, ic = args
        qpos = c * MOBA_QCHUNK + jnp.arange(MOBA_QCHUNK)
        own = (c * MOBA_QCHUNK) // MOBA_BLOCK
        qf = qc.astype(jnp.float32)
        kg = kbl[bi, hi, ic].astype(jnp.float32)
        vg = vbl[bi, hi, ic].astype(jnp.float32)
        ls = jnp.einsum('bhqd,bhqjkd->bhqjk', qf, kg) * ATTN_SCALE
        kpos_sel = ic[..., None] * MOBA_BLOCK + koff
        dist_sel = (qpos[None, None, :, None, None] - kpos_sel).astype(jnp.float32)
        valid = jnp.arange(top)[None, :] < (qpos // MOBA_BLOCK)[:, None]
        ls = jnp.where(valid[None, None, :, :, None], ls - sl5 * dist_sel, -jnp.inf)
        kown = lax.dynamic_index_in_dim(kbl, own, axis=2, keepdims=False).astype(jnp.float32)
        vown = lax.dynamic_index_in_dim(vbl, own, axis=2, keepdims=False).astype(jnp.float32)
        lo = jnp.einsum('bhqd,bhkd->bhqk', qf, kown) * ATTN_SCALE
        dist_own = qpos[:, None] - (own * MOBA_BLOCK + koff)[None, :]
        lo = jnp.where(dist_own >= 0, lo - sl4 * dist_own.astype(jnp.float32), -jnp.inf)
        logits = jnp.concatenate([ls.reshape(B, MOBA_HEADS, MOBA_QCHUNK, top * MOBA_BLOCK), lo], axis=-1)
        p = jax.nn.softmax(logits, axis=-1)
        ps = p[..., :top * MOBA_BLOCK].reshape(B, MOBA_HEADS, MOBA_QCHUNK, top, MOBA_BLOCK)
        po = p[..., top * MOBA_BLOCK:]
        return (jnp.einsum('bhqjk,bhqjkd->bhqd', ps, vg)
                + jnp.einsum('bhqk,bhkd->bhqd', po, vown))

    out = lax.map(chunk, (jnp.arange(nch), q_ch, idx_ch))
    out = out.transpose(1, 0, 3, 2, 4).reshape(B, sp, MOBA_HEADS * HEAD_DIM)[:, :S]
    return out.astype(q.dtype)


def hybrid_mixer(x, w_in, sinks, conv_w, w_br_a, w_br_b, w_br_c, w_out):
    B, S = x.shape[0], x.shape[1]
    proj = jnp.einsum('bsd,de->bse', x, w_in)
    a_q, a_k, a_v, b_h, b_b, b_c, c_q, c_k, c_v, g_a, g_b, g_c = jnp.split(proj, PROJ_SPLITS, axis=-1)
    slopes_a, slopes_c = alibi_slopes()
    y_a = sliding_window_attention(
        a_q.reshape(B, S, SWA_HEADS, HEAD_DIM),
        a_k.reshape(B, S, SWA_KV_HEADS, HEAD_DIM),
        a_v.reshape(B, S, SWA_KV_HEADS, HEAD_DIM), sinks, slopes_a)
    y_b = gated_short_conv(b_h, b_b, b_c, conv_w)
    y_c = moba_attention(
        c_q.reshape(B, S, MOBA_HEADS, HEAD_DIM),
        c_k.reshape(B, S, MOBA_HEADS, HEAD_DIM),
        c_v.reshape(B, S, MOBA_HEADS, HEAD_DIM), slopes_c)
    merged = (jax.nn.sigmoid(g_a) * (y_a @ w_br_a)
              + jax.nn.sigmoid(g_b) * (y_b @ w_br_b)
              + jax.nn.sigmoid(g_c) * (y_c @ w_br_c))
    return merged @ w_out


def swiglu(x, w_gate, w_up, w_down):
    return (jax.nn.silu(x @ w_gate) * (x @ w_up)) @ w_down


def setup_inputs(seed: int = 0) -> dict:
    key = jax.random.key(seed)
    ks = jax.random.split(key, 16)
    f32 = jnp.float32
    nrm = lambda k, shape, s: jax.random.normal(k, shape, f32) * s
    L = DEPTH
    br_in = SWA_HEADS * HEAD_DIM
    return {
        'x': jax.random.normal(ks[0], (BATCH, SEQ, D_MODEL), f32),
        'w_in': nrm(ks[1], (L, D_MODEL, PROJ_WIDTH), D_MODEL ** -0.5),
        'attn_sinks': nrm(ks[2], (L, SWA_HEADS), 0.5),
        'conv_w': nrm(ks[3], (L, SC_KSIZE, SC_WIDTH), SC_KSIZE ** -0.5),
        'w_branch_a': nrm(ks[4], (L, br_in, D_MODEL), br_in ** -0.5 * BETA),
        'w_branch_b': nrm(ks[5], (L, SC_WIDTH, D_MODEL), SC_WIDTH ** -0.5 * BETA),
        'w_branch_c': nrm(ks[6], (L, MOBA_HEADS * HEAD_DIM, D_MODEL), (MOBA_HEADS * HEAD_DIM) ** -0.5 * BETA),
        'w_out': nrm(ks[7], (L, D_MODEL, D_MODEL), D_MODEL ** -0.5 * BETA),
        'ln1_g': 1.0 + nrm(ks[8], (L, D_MODEL), 0.02),
        'ln1_b': nrm(ks[9], (L, D_MODEL), 0.02),
        'w_ffn_gate': nrm(ks[10], (L, D_MODEL, FFN_HIDDEN), D_MODEL ** -0.5 * BETA),
        'w_ffn_up': nrm(ks[11], (L, D_MODEL, FFN_HIDDEN), D_MODEL ** -0.5 * BETA),
        'w_ffn_down': nrm(ks[12], (L, FFN_HIDDEN, D_MODEL), FFN_HIDDEN ** -0.5 * BETA),
        'ln2_g': 1.0 + nrm(ks[13], (L, D_MODEL), 0.02),
        'ln2_b': nrm(ks[14], (L, D_MODEL), 0.02),
    }


def reference(x, w_in, attn_sinks, conv_w, w_branch_a, w_branch_b, w_branch_c, w_out,
              ln1_g, ln1_b, w_ffn_gate, w_ffn_up, w_ffn_down, ln2_g, ln2_b):
    for l in range(DEPTH):
        mix = hybrid_mixer(x, w_in[l], attn_sinks[l], conv_w[l], w_branch_a[l],
                           w_branch_b[l], w_branch_c[l], w_out[l])
        x = layer_norm(ALPHA * x + mix, ln1_g[l], ln1_b[l])
        ffn = swiglu(x, w_ffn_gate[l], w_ffn_up[l], w_ffn_down[l])
        x = layer_norm(ALPHA * x + ffn, ln2_g[l], ln2_b[l])
    return x
```

```python
import numpy as np
import concourse.bass as bass
import concourse.mybir as mybir
from concourse.bass_utils import run_bass_kernel_spmd

F32 = mybir.dt.float32
BF16 = mybir.dt.bfloat16
AF = mybir.ActivationFunctionType
ALU = mybir.AluOpType
AX = mybir.AxisListType

DM = 1024
T = 512
FFN = 2816
NHC = 22
ALPHA = 4.0 ** 0.25
LN_EPS = 1e-5
SCALE = 0.125
NEGB = -30000.0
NEGF = -1.0e30
SLOPES = [2.0 ** (-(i + 1) / 2.0) for i in range(16)]
SEM_MAX = 30000


class Ev:
    __slots__ = ("sem", "val", "key")

    def __init__(self, sem, val, key):
        self.sem = sem
        self.val = val
        self.key = key


class Chan:
    def __init__(self, P, name, step):
        self.P = P
        self.name = name
        self.step = step
        self.sem = None
        self.cnt = 0
        self.gen = 0
        self.last = None

    def next(self):
        if self.sem is None or self.cnt + self.step > SEM_MAX:
            self.sem = self.P.nc.alloc_semaphore("%s_%d" % (self.name, self.gen))
            self.key = "%s_%d" % (self.name, self.gen)
            self.gen += 1
            self.cnt = 0
        self.cnt += self.step
        self.last = Ev(self.sem, self.cnt, self.key)
        return self.last


class Buf:
    def __init__(self, name, t=None):
        self.name = name
        self.t = t
        self.w = {}
        self.r = {}

    def __getitem__(self, k):
        return self.t[k]


def _merge(d, ev):
    if ev is None:
        return
    o = d.get(ev.key)
    if o is None or o.val < ev.val:
        d[ev.key] = ev


def I(meth, *args, **kw):
    return lambda e: getattr(e, meth)(*args, **kw)


class Prog:
    ENGS = ("pe", "act", "dve", "pool", "sp")

    def __init__(self, nc):
        self.nc = nc
        self.q = {e: [] for e in self.ENGS}
        self.seen = {e: {} for e in self.ENGS}
        self.chan = {e: Chan(self, "c_" + e, 1) for e in ("pe", "act", "dve", "pool")}
        self.nops = 0

    def op(self, eng, fn, rd=(), wr=(), waits=(), sig=True, chan=None, skip_self=False):
        need = {}
        for b in rd:
            for ev in b.w.values():
                _merge(need, ev)
        for b in wr:
            for ev in b.w.values():
                _merge(need, ev)
            for ev in b.r.values():
                _merge(need, ev)
        for ev in waits:
            _merge(need, ev)
        ws = []
        seen = self.seen[eng]
        selfkey = self.chan[eng].key if (eng in self.chan and self.chan[eng].sem is not None) else None
        for key, ev in need.items():
            if skip_self and key == selfkey:
                continue
            if seen.get(key, 0) >= ev.val:
                continue
            seen[key] = ev.val
            ws.append(ev)
        evo = None
        step = 0
        if sig:
            ch = chan if chan is not None else self.chan[eng]
            evo = ch.next()
            step = ch.step
            for b in rd:
                _merge(b.r, evo)
            for b in wr:
                _merge(b.w, evo)
        self.q[eng].append((fn, ws, evo, step))
        self.nops += 1
        return evo

    def replay(self, eng, e):
        for fn, ws, evo, step in self.q[eng]:
            for w in ws:
                e.wait_ge(w.sem, w.val)
            ins = fn(e)
            if evo is not None:
                ins.then_inc(evo.sem, step)


def layer_blocks():
    B = []

    def win(c0, n):
        return (8, n, [("w_in", 0, c0, n, 0, 0, 8)])

    B.append(win(0, 512))
    B.append(win(512, 256))
    for c in range(4):
        B.append((8, 384, [("w_in", 0, 768 + c * 128, 128, 0, 0, 8),
                           ("w_in", 0, 1280 + c * 128, 128, 0, 128, 8),
                           ("w_in", 0, 1792 + c * 128, 128, 0, 256, 8)]))
    B.append(win(2816, 512))
    B.append(win(2304, 512))
    B.append(win(3328, 512))
    for fp in range(4):
        for br, nm in enumerate(("w_branch_a", "w_branch_b", "w_branch_c")):
            B.append((12, 256, [("w_in", 0, 3840 + br * 1024 + fp * 256, 256, 0, 0, 8),
                                (nm, 0, fp * 256, 256, 8, 0, 4)]))
    for hf in range(2):
        B.append((8, 512, [("w_out", 0, hf * 512, 512, 0, 0, 8)]))
    for hp in range(11):
        B.append((8, 512, [("w_ffn_gate", 0, hp * 256, 256, 0, 0, 8),
                           ("w_ffn_up", 0, hp * 256, 256, 0, 256, 8)]))
    for hf in range(2):
        for (r0, nk) in ((0, 8), (8, 8), (16, 6)):
            B.append((nk, 512, [("w_ffn_down", r0 * 128, hf * 512, 512, 0, 0, nk)]))
    return B


def build(S, L, stop_after=None, dumps=None):
    nc = bass.Bass("TRN2", target_bir_lowering=False)
    P = Prog(nc)
    NT = S // T
    NBK = S // 256
    dumps = dumps if dumps is not None else {}

    def din(name, shape):
        return nc.dram_tensor(name, shape, F32, kind="ExternalInput").ap()

    x_d = din("x", [S, DM])
    W = {
        "w_in": din("w_in", [L, DM, 6912]),
        "w_branch_a": din("w_branch_a", [L, 512, DM]),
        "w_branch_b": din("w_branch_b", [L, 512, DM]),
        "w_branch_c": din("w_branch_c", [L, 512, DM]),
        "w_out": din("w_out", [L, DM, DM]),
        "w_ffn_gate": din("w_ffn_gate", [L, DM, FFN]),
        "w_ffn_up": din("w_ffn_up", [L, DM, FFN]),
        "w_ffn_down": din("w_ffn_down", [L, FFN, DM]),
    }
    sinks_d = din("attn_sinks", [L, 8])
    convw_d = din("conv_w", [L, 3, 512])
    lnp = {n: din(n, [L, DM]) for n in ("ln1_g", "ln1_b", "ln2_g", "ln2_b")}
    out_d = nc.dram_tensor("out", [S, DM], F32, kind="ExternalOutput").ap()

    BLK = layer_blocks()
    NBLK = len(BLK)
    wsc = nc.dram_tensor("wsc", [L, NBLK, 128, 4096], BF16).ap()
    kcache = nc.dram_tensor("kcache", [L, NBK, 2, 96, 1024], BF16).ap()
    vcache = nc.dram_tensor("vcache", [L, NBK, 2, 128, 1024], BF16).ap()
    wsc_bufs = {(l, p_): Buf("wsc%d_%d" % (l, p_)) for l in range(L) for p_ in range(2)}

    def wsc_part(bi):
        return 0 if bi < 21 else 1
    kc_buf = [[Buf("kc%d_%d" % (l, j)) for j in range(NBK)] for l in range(L)]

    sb_off = [(nc.sbuf_base + 63) // 64 * 64]
    sb_top = nc.sbuf_top

    def alloc(name, shape, dtype, at=None):
        nb = int(np.prod(shape[1:])) * (4 if dtype == F32 else 2)
        nb = (nb + 63) // 64 * 64
        if at is None:
            off = sb_off[0]
            sb_off[0] += nb
            assert sb_off[0] <= sb_top, "SBUF overflow at %s: %d > %d" % (name, sb_off[0], sb_top)
        else:
            off = at
        t = nc.alloc_sbuf_tensor_at(name, list(shape), dtype, offset=off)
        b = Buf(name, t)
        b.off = off
        b.nb = nb
        return b

    ident = alloc("ident", [128, 128], BF16)
    tri = alloc("tri", [128, 128], BF16)
    Dq = alloc("Dq", [128, 128], F32)
    btab = alloc("btab", [128, 8, 66], F32)
    pm2 = alloc("pm2", [128, 64], F32)
    swab0 = alloc("swab0", [128, 2, 512], F32)
    swab1 = alloc("swab1", [128, 2, 512], F32)
    esink = alloc("esink", [128, L, 8], F32)
    convw = alloc("convw", [128, L, 4, 3], F32)
    epsb = alloc("epsb", [128, 1], F32)
    ohtab = alloc("ohtab", [128, 32], F32)
    kmf = alloc("kmf", [64, L, 8, 32], F32)
    kmh = alloc("kmh", [64, L, 2, 8, 32], BF16)
    uhalo = alloc("uhalo", [128, L, 4, 2], F32)
    kahalo = alloc("kahalo", [64, L, 2, 128], BF16)
    vahalo = alloc("vahalo", [128, L, 2, 128], BF16)
    xr = [alloc("xr%d" % i, [128, DM], F32) for i in range(4)]
    xbs = [alloc("xb%d" % i, [128, DM], BF16) for i in range(2)]
    xb_n = [0]
    xT = [alloc("xT%d" % i, [128, 8, T], BF16) for i in range(2)]
    lnpar = [alloc("lnpar%d" % i, [128, DM], F32) for i in range(2)]
    lnts = [alloc("lnt%d" % i, [128, DM], F32) for i in range(3)]
    lnt = lnts[0]
    lnst = alloc("lnst", [128, 4, 12], F32)
    lnmv = alloc("lnmv", [128, 4, 2], F32)
    lnrs = alloc("lnrs", [128, 4], F32)
    lnnm = alloc("lnnm", [128, 4], F32)
    ring = [alloc("ring%d" % i, [128, 4096], BF16) for i in range(3)]
    qTa = alloc("qTa", [64, 8, T], BF16)
    kTa = alloc("kTa", [64, 2, 128 + T], BF16)
    va = alloc("va", [128, 5, 2, 128], BF16)
    yaT = alloc("yaT", [128, 4, T], BF16)
    ybT = alloc("ybT", [128, 4, T], BF16)
    ycT = alloc("ycT", [128, 4, T], BF16)
    ubuf = alloc("ubuf", [128, 4, T + 2], F32)
    f32tmp = [alloc("f32tmp%d" % i, [128, T], F32) for i in range(4)]
    pT = [alloc("pT%d" % i, [128, T], BF16) for i in range(4)]
    rec = [alloc("rec%d" % i, [64, T], F32) for i in range(2)]
    gtmp = alloc("gtmp", [128, 8, 32], F32)
    ltmp = alloc("ltmp", [128, 8, 32], F32)
    m8 = alloc("m8", [128, 8, 8], F32)
    mbs = [alloc("mb%d" % i, [128, 8, 32], BF16) for i in range(4)]
    stg = [(alloc("stgk%d" % i, [96, 4, 256], BF16), alloc("stgv%d" % i, [128, 2, 4, 128], BF16)) for i in range(3)]
    u0 = sb_off[0]
    qTg = alloc("qTg", [96, 8, T], BF16)
    kTg = alloc("kTg", [96, 8, T], BF16)
    vg = alloc("vg", [128, 4, 8, 128], BF16)
    u1 = sb_off[0]
    assert u1 - u0 >= NHC * T * 2
    hT = alloc("hT", [128, NHC, T], BF16, at=u0)
    hT_al = [qTg, kTg, vg]
    mergedT = alloc("mergedT", [128, 8, T], BF16)
    sg = [alloc("sg%d" % i, [128, T], F32) for i in range(6)]
    print("SBUF used %d / %d" % (sb_off[0], sb_top))

    banks_t = [nc.alloc_psum_tensor("bank%d" % i, [128, 512], F32) for i in range(8)]
    banks = [Buf("bank%d" % i, banks_t[i]) for i in range(8)]
    bank_rr = [0]

    reserved = set()

    def bank():
        while True:
            i = bank_rr[0] % 8
            bank_rr[0] += 1
            if i not in reserved:
                return banks[i]

    def reserve(n):
        out = []
        for _ in range(n):
            b = bank()
            reserved.add(banks.index(b))
            out.append(b)
        return out

    def unreserve(bs):
        for b in bs:
            reserved.discard(banks.index(b))

    pre_chans = {(l, p_): Chan(P, "pre%d_%d" % (l, p_), 16) for l in range(L) for p_ in range(2)}
    ring_chan = [Chan(P, "ring%d" % i, 16) for i in range(3)]
    stg_chan = [Chan(P, "stg%d" % i, 16) for i in range(3)]
    ld_chan = Chan(P, "ld", 16)
    xld_chan = [Chan(P, "xld%d" % i, 16) for i in range(4)]
    par_chan = [Chan(P, "par%d" % i, 16) for i in range(2)]
    st_chans = [Chan(P, "st%d" % i, 16) for i in range(8)]
    st_rr = [0]

    def dma(eng, out_ap, in_ap, rd, wr, chan, waits=()):
        ws = list(waits)
        if chan.last is not None:
            ws.append(chan.last)
        return P.op(eng, I("dma_start", out=out_ap, in_=in_ap), rd=rd, wr=wr, waits=ws, chan=chan)

    def store(eng, out_ap, in_ap, rd, wr):
        ch = st_chans[st_rr[0] % len(st_chans)]
        st_rr[0] += 1
        return dma(eng, out_ap, in_ap, rd, wr, ch)

    dump_specs = []

    def dump(name, buf, ap, shape, dtype):
        if name not in dumps:
            return
        d = nc.dram_tensor("dbg_" + name, list(shape), dtype, kind="ExternalOutput").ap()
        store("pool", d, ap, [buf], [])
        dump_specs.append(name)

    def pool(fn, rd=(), wr=(), waits=()):
        return P.op("pool", fn, rd=rd, wr=wr, waits=waits)

    def dve(fn, rd=(), wr=(), waits=()):
        return P.op("dve", fn, rd=rd, wr=wr, waits=waits)

    def act(fn, rd=(), wr=(), waits=()):
        return P.op("act", fn, rd=rd, wr=wr, waits=waits)

    pool(I("iota", Dq[:], pattern=[[1, 128]], base=0, channel_multiplier=-1,
                          allow_small_or_imprecise_dtypes=True), wr=[Dq])
    pool(I("tensor_single_scalar", out=ident[:], in_=Dq[:], scalar=0.0, op=ALU.is_equal), rd=[Dq], wr=[ident])
    pool(I("tensor_scalar", out=tri[:], in0=Dq[:], scalar1=0.0, scalar2=NEGB, op0=ALU.is_lt, op1=ALU.mult),
         rd=[Dq], wr=[tri])
    pool(I("iota", lnt[:, 0:66], pattern=[[-128, 66]], base=384, channel_multiplier=1,
                          allow_small_or_imprecise_dtypes=True), wr=[lnt])
    for h in range(8):
        pool(I("tensor_scalar", out=btab[:, h, :], in0=lnt[:, 0:66], scalar1=SLOPES[8 + h], scalar2=None,
                                            op0=ALU.mult), rd=[lnt], wr=[btab])
    pool(I("iota", pm2[:], pattern=[[1, 64]], base=-32, channel_multiplier=0,
           allow_small_or_imprecise_dtypes=True), wr=[pm2])
    pool(I("tensor_scalar", out=pm2[:], in0=pm2[:], scalar1=0.0, scalar2=NEGF, op0=ALU.is_ge, op1=ALU.mult),
         rd=[pm2], wr=[pm2])
    for g in range(2):
        for hl in range(4):
            sl = SLOPES[g * 4 + hl]
            cs = slice(hl * 128, (hl + 1) * 128)
            pool(I("tensor_scalar", out=swab1[:, g, cs], in0=Dq[:], scalar1=0.0, scalar2=NEGF,
                                                        op0=ALU.is_lt, op1=ALU.mult), rd=[Dq], wr=[swab1])
            pool(I("tensor_scalar", out=lnt[:, 128:256], in0=Dq[:], scalar1=-sl, scalar2=None, op0=ALU.mult),
                 rd=[Dq], wr=[lnt])
            pool(I("tensor_tensor", out=swab1[:, g, cs], in0=swab1[:, g, cs], in1=lnt[:, 128:256],
                                                        op=ALU.add), rd=[lnt, swab1], wr=[swab1])
            dve(I("tensor_scalar", out=swab0[:, g, cs], in0=Dq[:], scalar1=0.0, scalar2=NEGF,
                  op0=ALU.is_ge, op1=ALU.mult), rd=[Dq], wr=[swab0])
            dve(I("tensor_scalar", out=lnts[1][:, 256:384], in0=Dq[:], scalar1=128.0, scalar2=-sl,
                  op0=ALU.add, op1=ALU.mult), rd=[Dq], wr=[lnts[1]])
            dve(I("tensor_tensor", out=swab0[:, g, cs], in0=swab0[:, g, cs], in1=lnts[1][:, 256:384],
                  op=ALU.add), rd=[lnts[1], swab0], wr=[swab0])
    pool(I("memset", epsb[:], LN_EPS), wr=[epsb])
    pool(I("memset", ohtab[:], 0.0), wr=[ohtab])
    pool(I("iota", ohtab[64:96, :], pattern=[[1, 32]], base=0, channel_multiplier=-1,
           allow_small_or_imprecise_dtypes=True), wr=[ohtab])
    pool(I("tensor_single_scalar", out=ohtab[64:96, :], in_=ohtab[64:96, :], scalar=0.0, op=ALU.is_equal),
         rd=[ohtab], wr=[ohtab])
    pool(I("memset", kTg[:], 0.0), wr=[kTg])
    pool(I("memset", kmh[:], 0.0), wr=[kmh])
    pool(I("memset", kmf[:], 0.0), wr=[kmf])
    pool(I("memset", uhalo[:], 0.0), wr=[uhalo])
    pool(I("memset", kahalo[:], 0.0), wr=[kahalo])
    pool(I("memset", vahalo[:], 0.0), wr=[vahalo])
    pool(I("memset", va[:], 1.0), wr=[va])
    pool(I("memset", vg[:], 1.0), wr=[vg])
    pool(I("memset", qTg[:], 0.0), wr=[qTg])
    dma("sp", esink[:].rearrange("p l h -> p (l h)"), sinks_d.rearrange("l h -> (l h)").partition_broadcast(128),
        [], [esink], ld_chan)
    act(I("activation", out=esink[:], in_=esink[:], func=AF.Exp), rd=[esink], wr=[esink])
    for l in range(L):
        for c in range(4):
            for k in range(3):
                dma("sp", convw[:, l, c, k:k + 1], convw_d[l, k, c * 128:(c + 1) * 128].rearrange("(p o) -> p o", o=1),
                    [], [convw], ld_chan)

    for s4 in range(4):
        dma("pool", xr[s4][:], x_d[s4 * 128:(s4 + 1) * 128, :], [], [xr[s4]], xld_chan[s4])
    def emit_prepass(l):
        for bi, (kc, ncols, parts) in enumerate(BLK):
            dst = wsc[l, bi, :, 0:kc * ncols].rearrange("p (k n) -> p k n", k=kc)
            for (src, r0, c0, n, dk, dc, nk) in parts:
                s_ap = W[src][l, r0:r0 + nk * 128, c0:c0 + n].rearrange("(k p) n -> p k n", p=128)
                d_ap = dst[:, dk:dk + nk, dc:dc + n]
                pre_ev = P.op("pool", I("dma_start", out=d_ap, in_=s_ap), chan=pre_chans[(l, wsc_part(bi))])
                _merge(wsc_bufs[(l, wsc_part(bi))].w, pre_ev)

    emit_prepass(0)

    class WB:
        pass

    ring_free = [True, True, True]
    seq = [(t, l, bi) for t in range(NT) for l in range(L) for bi in range(NBLK)]
    seq_pos = [0]
    pending = []

    def issue_one():
        free = [i for i in range(3) if ring_free[i]]
        if not free or seq_pos[0] >= len(seq):
            return False
        slot = free[0]
        ring_free[slot] = False
        _, l, bi = seq[seq_pos[0]]
        seq_pos[0] += 1
        kc, ncols, _ = BLK[bi]
        rb = ring[slot]
        dma("sp", rb[:, 0:kc * ncols], wsc[l, bi, :, 0:kc * ncols], [wsc_bufs[(l, wsc_part(bi))]], [rb], ring_chan[slot])
        w = WB()
        w.slot = slot
        w.buf = rb
        w.v = rb[:, 0:kc * ncols].rearrange("p (k n) -> p k n", k=kc)
        pending.append((l, bi, w))
        return True

    def next_block(l, bi):
        while not pending:
            assert issue_one()
        ll, bb, w = pending.pop(0)
        assert (ll, bb) == (l, bi), ((ll, bb), (l, bi))
        return w

    def release(w):
        ring_free[w.slot] = True
        while issue_one():
            pass

    def mm_group(mms, rd, wr):
        n = len(mms)
        ev = None
        for i, fn in enumerate(mms):
            ev = P.op("pe", fn, rd=(rd if i == 0 else ()), wr=(wr if i == 0 else ()), sig=(i == n - 1), skip_self=True)
        for b in rd:
            _merge(b.r, ev)
        for b in wr:
            _merge(b.w, ev)
        return ev

    def proj_fm(w, c0, ncol, xt, pb, prow=None):
        o = pb[0:ncol, :]
        mms = [(I("matmul", o, lhsT=w.v[:, k, c0:c0 + ncol], rhs=xt[:, k, :], start=(k == 0), stop=(k == 7)))
               for k in range(8)]
        return mm_group(mms, [w.buf, xt], [pb])

    def transpose_in(l, xtb):
        for s in range(4):
            xb = xbs[xb_n[0] % 2]
            xb_n[0] += 1
            act(I("activation", out=xb[:], in_=xr[s][:], func=AF.Copy), rd=[xr[s]], wr=[xb])
            pb = bank()
            pbv = pb[:].bitcast(BF16)
            mms = [(I("transpose", out=pbv[:, k * 128:(k + 1) * 128], in_=xb[:, k * 128:(k + 1) * 128],
                      identity=ident[:])) for k in range(8)]
            mm_group(mms, [xb, ident], [pb])
            dve(I("tensor_copy", out=xtb[:, :, s * 128:(s + 1) * 128],
                  in_=pbv.rearrange("p (k t) -> p k t", k=8)), rd=[pb], wr=[xtb])

    def ln_load_params(l, gname, bname):
        dma("sp", lnpar[0][:], lnp[gname][l].partition_broadcast(128), [], [lnpar[0]], par_chan[0])
        dma("sp", lnpar[1][:], lnp[bname][l].partition_broadcast(128), [], [lnpar[1]], par_chan[1])

    def ln_stats_half(s, hf):
        dve(I("bn_stats", out=lnst[:, s, hf * 6:hf * 6 + 6], in_=xr[s][:, hf * 512:(hf + 1) * 512]), rd=[xr[s]], wr=[lnst])

    def ln_aggr(s):
        dve(I("bn_aggr", out=lnmv[:, s, :], in_=lnst[:, s, :]), rd=[lnst], wr=[lnmv])

    def ln_stats(s):
        ln_stats_half(s, 0)
        ln_stats_half(s, 1)
        ln_aggr(s)

    def ln_finish(next_xt):
        act(I("activation", out=lnrs[:], in_=lnmv[:, :, 1], func=AF.Sqrt, bias=epsb[:], scale=1.0),
            rd=[lnmv, epsb], wr=[lnrs])
        dve(I("reciprocal", out=lnrs[:], in_=lnrs[:]), rd=[lnrs], wr=[lnrs])
        dve(I("scalar_tensor_tensor", out=lnnm[:], in0=lnmv[:, :, 0], scalar=-1.0, in1=lnrs[:], op0=ALU.mult, op1=ALU.mult),
            rd=[lnmv, lnrs], wr=[lnnm])
        pend_ev = None
        for s in range(4):
            lt = lnts[s % 3]
            act(I("activation", out=lt[:], in_=xr[s][:], func=AF.Identity, bias=lnnm[:, s:s + 1],
                  scale=lnrs[:, s:s + 1]), rd=[xr[s], lnnm, lnrs], wr=[lt])
            dve(I("tensor_tensor", out=lt[:], in0=lt[:], in1=lnpar[0][:], op=ALU.mult), rd=[lt, lnpar[0]], wr=[lt])
            if next_xt is not None:
                xb = xbs[xb_n[0] % 2]
                xb_n[0] += 1
                dve(I("tensor_tensor", out=xb[:], in0=lt[:], in1=lnpar[1][:], op=ALU.add), rd=[lt, lnpar[1]], wr=[xb])
            pool(I("tensor_tensor", out=xr[s][:], in0=lt[:], in1=lnpar[1][:], op=ALU.add),
                 rd=[lt, lnpar[1]], wr=[xr[s]])
            if next_xt is not None:
                pb = bank()
                pbv = pb[:].bitcast(BF16)
                mms = [(I("transpose", out=pbv[:, k * 128:(k + 1) * 128], in_=xb[:, k * 128:(k + 1) * 128],
                          identity=ident[:])) for k in range(8)]
                mm_group(mms, [xb, ident], [pb])
                if pend_ev is not None:
                    pend_ev()
                pend_ev = (lambda pb=pb, pbv=pbv, s=s: act(I("activation", out=next_xt[:, :, s * 128:(s + 1) * 128],
                           in_=pbv.rearrange("p (k t) -> p k t", k=8), func=AF.Copy), rd=[pb], wr=[next_xt]))
        if pend_ev is not None:
            pend_ev()

    def tile_layer(t, l, xtb, xtb2):
        c0b = 2 * t
        stage = [0]

        def done(name):
            return stop_after is not None and stop_after == (t, l, name)

        if done("init"):
            return False
        LA = 2
        pti = [0]
        units = [(hh_, j_) for hh_ in range(2) for j_ in range(c0b)]
        unit_slot = {}
        next_unit = [0]

        def load_unit():
            if next_unit[0] >= len(units):
                return
            hh_, j_ = units[next_unit[0]]
            next_unit[0] += 1
            slot = stage[0] % 3
            stage[0] += 1
            sk, sv = stg[slot]
            dma("sp", sk[:].rearrange("p h k -> p (h k)"), kcache[l, j_, hh_], [kc_buf[l][j_]], [sk], stg_chan[slot])
            dma("sp", sv[:].rearrange("p c h d -> p (c h d)"), vcache[l, j_, hh_], [kc_buf[l][j_]], [sv], stg_chan[slot])
            unit_slot[(hh_, j_)] = (sk, sv)

        for _ in range(3):
            load_unit()
        w = next_block(l, 0)
        for hp in range(4):
            pb = bank()
            proj_fm(w, hp * 128, 128, xtb, pb)
            act(I("activation", out=qTa[:, 2 * hp, :], in_=pb[0:64, :], func=AF.Copy), rd=[pb], wr=[qTa])
            act(I("activation", out=qTa[:, 2 * hp + 1, :], in_=pb[64:128, :], func=AF.Copy), rd=[pb], wr=[qTa])
        release(w)
        w = next_block(l, 1)
        pool(I("tensor_copy", out=kTa[:, :, 0:128], in_=kahalo[:, l, :, :]), rd=[kahalo], wr=[kTa])
        pool(I("tensor_copy", out=va[:, 0, :, 0:64], in_=vahalo[:, l, :, 0:64]), rd=[vahalo], wr=[va])
        pb = bank()
        proj_fm(w, 0, 128, xtb, pb)
        for g in range(2):
            act(I("activation", out=kTa[:, g, 128:128 + T], in_=pb[g * 64:(g + 1) * 64, :], func=AF.Copy), rd=[pb], wr=[kTa])
        pb = bank()
        for s in range(4):
            o = pb[:, s * 128:(s + 1) * 128]
            mms = [(I("matmul", o, lhsT=xtb[:, k, s * 128:(s + 1) * 128], rhs=w.v[:, k, 128:256],
                                                       start=(k == 0), stop=(k == 7))) for k in range(8)]
            mm_group(mms, [w.buf, xtb], [pb])
        dve(I("tensor_copy", out=va[:, 1:5, :, 0:64],
                                           in_=pb[:, :].rearrange("p (s g d) -> p s g d", s=4, g=2)), rd=[pb], wr=[va])
        pool(I("tensor_copy", out=kahalo[:, l, :, :], in_=kTa[:, :, T:T + 128]), rd=[kTa], wr=[kahalo])
        pool(I("tensor_copy", out=vahalo[:, l, :, 0:64], in_=va[:, 4, :, 0:64]), rd=[va], wr=[vahalo])
        release(w)
        if done("proj"):
            return False
        if "qTa" in dumps and (t, l) == dumps["qTa"]:
            dump("qTa", qTa, qTa[:], [64, 8, T], BF16)
            dump("kTa", kTa, kTa[:], [64, 2, 128 + T], BF16)
            dump("va", va, va[:], [128, 5, 2, 128], BF16)

        def fill_gen():
            dve(I("tensor_copy", out=ubuf[:, :, 0:2], in_=uhalo[:, l, :, :]), rd=[uhalo], wr=[ubuf])
            for c in range(4):
                w = next_block(l, 2 + c)
                ph = bank()
                proj_fm(w, 0, 128, xtb, ph)
                hs = f32tmp[0]
                act(I("activation", out=hs[:], in_=ph[:, :], func=AF.Copy), rd=[ph], wr=[hs])
                pc_ = bank()
                proj_fm(w, 256, 128, xtb, pc_)
                dve(I("tensor_tensor", out=ubuf[:, c, 2:T + 2], in0=pc_[:, :], in1=hs[:], op=ALU.mult),
                    rd=[pc_, hs], wr=[ubuf])
                acc = f32tmp[1]
                dve(I("tensor_scalar", out=acc[:], in0=ubuf[:, c, 0:T], scalar1=convw[:, l, c, 0:1],
                                                            scalar2=None, op0=ALU.mult), rd=[ubuf, convw], wr=[acc])
                dve(I("scalar_tensor_tensor", out=acc[:], in0=ubuf[:, c, 1:T + 1], scalar=convw[:, l, c, 1:2],
                                                                   in1=acc[:], op0=ALU.mult, op1=ALU.add), rd=[ubuf, convw, acc], wr=[acc])
                dve(I("scalar_tensor_tensor", out=acc[:], in0=ubuf[:, c, 2:T + 2], scalar=convw[:, l, c, 2:3],
                                                                   in1=acc[:], op0=ALU.mult, op1=ALU.add), rd=[ubuf, convw, acc], wr=[acc])
                pB = bank()
                proj_fm(w, 128, 128, xtb, pB)
                release(w)
                dve(I("tensor_tensor", out=ybT[:, c, :], in0=pB[:, :], in1=acc[:], op=ALU.mult),
                    rd=[pB, acc], wr=[ybT])
                yield
            dve(I("tensor_copy", out=uhalo[:, l, :, :], in_=ubuf[:, :, T:T + 2]), rd=[ubuf], wr=[uhalo])

            w = next_block(l, 6)
            for hp in range(4):
                pb = bank()
                proj_fm(w, hp * 128, 128, xtb, pb)
                act(I("activation", out=kTg[0:64, 2 * hp, :], in_=pb[0:64, :], func=AF.Copy), rd=[pb], wr=[kTg])
                eva = act(I("activation", out=kTg[0:64, 2 * hp + 1, :], in_=pb[64:128, :], func=AF.Copy), rd=[pb], wr=[kTg])
                for a2 in range(2):
                    dve(I("tensor_reduce", out=kmf[:, l, 2 * hp + a2, c0b:c0b + 2],
                          in_=pb[a2 * 64:(a2 + 1) * 64, :].rearrange("p (b k) -> p b k", b=2),
                          axis=AX.X, op=ALU.add), rd=[pb], wr=[kmf], waits=[eva])
                if hp == 1:
                    yield
            release(w)
            dve(I("tensor_scalar", out=kmf[:, l, :, c0b:c0b + 2], in0=kmf[:, l, :, c0b:c0b + 2], scalar1=1.0 / 256.0,
                                          scalar2=None, op0=ALU.mult), rd=[kmf], wr=[kmf])
            dve(I("tensor_copy", out=kmh[:, l, 0, :, c0b:c0b + 2], in_=kmf[:, l, :, c0b:c0b + 2]), rd=[kmf], wr=[kmh])
            dve(I("tensor_tensor", out=kmh[:, l, 1, :, c0b:c0b + 2], in0=kmf[:, l, :, c0b:c0b + 2],
                                          in1=kmh[:, l, 0, :, c0b:c0b + 2], op=ALU.subtract), rd=[kmf, kmh], wr=[kmh])
            for blk in range(2):
                dve(I("tensor_scalar", out=kTg[64:96, :, blk * 256:(blk + 1) * 256].rearrange("p (c a) k -> p c a k", a=2),
                      in0=ubuf[64:96, :, 0:512].rearrange("p c (a k) -> p c a k", a=2),
                      scalar1=0.0, scalar2=ohtab[64:96, c0b + blk:c0b + blk + 1], op0=ALU.mult, op1=ALU.add),
                    rd=[ubuf, ohtab], wr=[kTg])
            w = next_block(l, 7)
            for hp in range(4):
                pb = bank()
                proj_fm(w, hp * 128, 128, xtb, pb)
                act(I("activation", out=qTg[0:64, 2 * hp, :], in_=pb[0:64, :], func=AF.Copy), rd=[pb], wr=[qTg])
                act(I("activation", out=qTg[0:64, 2 * hp + 1, :], in_=pb[64:128, :], func=AF.Copy), rd=[pb], wr=[qTg])
                if hp == 1:
                    yield
            release(w)
            yield
            pgs = []
            for s in range(4):
                pg = bank()
                mms = []
                for h in range(8):
                    for hi in range(2):
                        mms.append(I("matmul", pg[:, h * 32:(h + 1) * 32], lhsT=qTg[0:64, h, s * 128:(s + 1) * 128],
                                     rhs=kmh[:, l, hi, h, :], start=(hi == 0), stop=(hi == 1)))
                mm_group(mms, [qTg, kmh], [pg])
                pgs.append(pg)
            for s in range(4):
                c = c0b + (s // 2)
                pg = pgs[s]
                dve(I("tensor_tensor", out=gtmp[:], in0=pg[:, 0:256].rearrange("p (h n) -> p h n", h=8),
                      in1=pm2[:, 32 - c:64 - c].rearrange("p (o n) -> p o n", o=1).to_broadcast([128, 8, 32]), op=ALU.add),
                    rd=[pg, pm2], wr=[gtmp])
                for h in range(8):
                    dve(I("max", out=m8[:, h, :], in_=gtmp[:, h, :]), rd=[gtmp], wr=[m8])
                dve(I("tensor_tensor", out=ltmp[:], in0=gtmp[:], in1=m8[:, :, 2:3].to_broadcast([128, 8, 32]), op=ALU.is_lt),
                    rd=[gtmp, m8], wr=[ltmp])
                dve(I("tensor_scalar", out=mbs[s][:], in0=ltmp[:], scalar1=NEGB, scalar2=None, op0=ALU.mult), rd=[ltmp], wr=[mbs[s]])
            pool(I("memset", vg[:, :, :, 64:128], 1.0), wr=[vg])
            w = next_block(l, 8)
            for s in range(4):
                pb = bank()
                o = pb[:, :]
                mms = [(I("matmul", o, lhsT=xtb[:, k, s * 128:(s + 1) * 128], rhs=w.v[:, k, :],
                                                           start=(k == 0), stop=(k == 7))) for k in range(8)]
                mm_group(mms, [w.buf, xtb], [pb])
                dve(I("tensor_copy", out=vg[:, s, :, 0:64], in_=pb[:, :].rearrange("p (h d) -> p h d", h=8)),
                    rd=[pb], wr=[vg])
            release(w)
            for s in range(4):
                pt_ = bank()
                ptv = pt_[:].bitcast(BF16)
                mms = [(I("transpose", out=ptv[64:96, h * 128:(h + 1) * 128], in_=mbs[s][:, h, :], identity=ident[:]))
                       for h in range(8)]
                mm_group(mms, [mbs[s], ident], [pt_])
                dve(I("tensor_copy", out=qTg[64:96, :, s * 128:(s + 1) * 128],
                      in_=ptv[64:96, :].rearrange("p (h q) -> p h q", h=8)), rd=[pt_], wr=[qTg])
            if t < NT - 1:
                for blk in range(2):
                    for hh in range(2):
                        store("pool", kcache[l, c0b + blk, hh].rearrange("p (h k) -> p h k", h=4),
                              kTg[:, hh * 4:(hh + 1) * 4, blk * 256:(blk + 1) * 256], [kTg], [kc_buf[l][c0b + blk]])
                        store("pool", vcache[l, c0b + blk, hh].rearrange("p (c h d) -> p c h d", c=2, h=4),
                              vg[:, 2 * blk:2 * blk + 2, hh * 4:(hh + 1) * 4, :], [vg], [kc_buf[l][c0b + blk]])

            yield

        def swa_stage1(b, g):
            gb = 4 * t + b
            chunks = ([0] if gb > 0 else []) + [1]
            pTs = []
            for pc in chunks:
                pb = bank()
                kcols = slice(b * 128 + (0 if pc == 0 else 128), b * 128 + (128 if pc == 0 else 256))
                mms = []
                for hl in range(4):
                    h = g * 4 + hl
                    mms.append(I("matmul", pb[:, hl * 128:(hl + 1) * 128], lhsT=kTa[:, g, kcols],
                                 rhs=qTa[:, h, b * 128:(b + 1) * 128], start=True, stop=True))
                mm_group(mms, [kTa, qTa], [pb])
                ft = f32tmp[(b * 4 + g * 2 + pc) % 4]
                sw_ = swab0 if pc == 0 else swab1
                dve(I("scalar_tensor_tensor", out=ft[:], in0=pb[:, :], scalar=SCALE, in1=sw_[:, g, :],
                      op0=ALU.mult, op1=ALU.add), rd=[pb, sw_], wr=[ft])
                pt = pT[(b * 4 + g * 2 + pc) % 4]
                act(I("activation", out=pt[:], in_=ft[:], func=AF.Exp), rd=[ft], wr=[pt])
                pTs.append((pc, pt))
            return pTs

        def swa_stage2(b, g, pTs):
            po = bank()
            mms = []
            for i, (pc, pt) in enumerate(pTs):
                vblk = b + pc
                mms.append(I("matmul", po[:, :], lhsT=va[:, vblk, g, :], rhs=pt[:], start=(i == 0), stop=(i == len(pTs) - 1)))
            mm_group(mms, [va] + [p_[1] for p_ in pTs], [po])
            rc = rec[(b * 2 + g) % 2]
            for hl in range(4):
                h = g * 4 + hl
                dve(I("tensor_scalar", out=rc[0:64, hl * 128:(hl + 1) * 128], in0=po[64:128, hl * 128:(hl + 1) * 128],
                      scalar1=esink[64:128, l, h:h + 1], scalar2=None, op0=ALU.add), rd=[po, esink], wr=[rc])
            dve(I("reciprocal", out=rc[0:64, :], in_=rc[0:64, :]), rd=[rc], wr=[rc])
            for par in range(2):
                dve(I("tensor_tensor", out=yaT[par * 64:(par + 1) * 64, 2 * g:2 * g + 2, b * 128:(b + 1) * 128],
                      in0=po[0:64, :].rearrange("p (c a q) -> p c a q", c=2, a=2)[:, :, par, :],
                      in1=rc[0:64, :].rearrange("p (c a q) -> p c a q", c=2, a=2)[:, :, par, :],
                      op=ALU.mult), rd=[po, rc], wr=[yaT])

        filler = fill_gen()
        pend = None
        for b in range(4):
            for g in range(2):
                r_ = swa_stage1(b, g)
                next(filler, None)
                if pend is not None:
                    swa_stage2(*pend)
                pend = (b, g, r_)
        next(filler, None)
        swa_stage2(*pend)
        for _ in filler:
            pass
        if "yaT" in dumps and (t, l) == dumps["yaT"]:
            dump("yaT", yaT, yaT[:], [128, 4, T], BF16)
        if done("swa"):
            return False
        if "qTg" in dumps and (t, l) == dumps["qTg"]:
            dump("qTg", qTg, qTg[:], [96, 8, T], BF16)
            dump("kTg", kTg, kTg[:], [96, 8, T], BF16)
            dump("vg", vg, vg[:], [128, 4, 8, 128], BF16)
            dump("gtmp", gtmp, gtmp[:], [128, 8, 32], F32)

        if done("mgate"):
            return False
        if t == 0 and l == 0:
            for l2 in range(1, L):
                emit_prepass(l2)
        for hh in range(2):
            obanks = reserve(4)
            tasks = [("past", j, hl, ch) for j in range(c0b) for hl in range(4) for ch in range(2)]
            tasks += [("own", sub, hl, 0) for hl in range(4) for sub in range(4)]

            def emit_st(task):
                kind, a, hl, ch = task
                h = hh * 4 + hl
                ps_ = bank()
                if kind == "past":
                    sk, sv = unit_slot[(hh, a)]
                    mm_group([I("matmul", ps_[:, :], lhsT=sk[:, hl, ch * 128:(ch + 1) * 128], rhs=qTg[:, h, :], start=True, stop=True)],
                             [sk, qTg], [ps_])
                elif a == 0:
                    mm_group([
                        I("matmul", ps_[:, 0:256], lhsT=kTg[0:64, h, 0:128], rhs=qTg[0:64, h, 0:256], start=True, stop=False),
                        I("matmul", ps_[:, 0:128], lhsT=ident[:], rhs=tri[:], start=False, stop=False),
                        I("matmul", ps_[:, 256:512], lhsT=kTg[:, h, 0:128], rhs=qTg[:, h, 256:512], start=False, stop=True),
                    ], [kTg, qTg, ident, tri], [ps_])
                elif a == 1:
                    mm_group([
                        I("matmul", ps_[:, 128:256], lhsT=kTg[0:64, h, 128:256], rhs=qTg[0:64, h, 128:256], start=True, stop=False),
                        I("matmul", ps_[:, 128:256], lhsT=ident[:], rhs=tri[:], start=False, stop=False),
                        I("matmul", ps_[:, 256:512], lhsT=kTg[:, h, 128:256], rhs=qTg[:, h, 256:512], start=False, stop=True),
                    ], [kTg, qTg, ident, tri], [ps_])
                elif a == 2:
                    mm_group([
                        I("matmul", ps_[:, 256:512], lhsT=kTg[0:64, h, 256:384], rhs=qTg[0:64, h, 256:512], start=True, stop=False),
                        I("matmul", ps_[:, 256:384], lhsT=ident[:], rhs=tri[:], start=False, stop=True),
                    ], [kTg, qTg, ident, tri], [ps_])
                else:
                    mm_group([
                        I("matmul", ps_[:, 384:512], lhsT=kTg[0:64, h, 384:512], rhs=qTg[0:64, h, 384:512], start=True, stop=False),
                        I("matmul", ps_[:, 384:512], lhsT=ident[:], rhs=tri[:], start=False, stop=True),
                    ], [kTg, qTg, ident, tri], [ps_])
                return ps_

            def emit_ep(task, ps_):
                kind, a, hl, ch = task
                h = hh * 4 + hl
                if kind == "past":
                    sk, sv = unit_slot[(hh, a)]
                    cols = slice(0, T)
                    mi = (4 * t - 2 * a - ch) + 3
                    vl, vb = sv[:, ch, hl, :], [sv]
                    first = (a == 0 and ch == 0)
                    last = False
                else:
                    cols = (slice(0, 512), slice(128, 512), slice(256, 512), slice(384, 512))[a]
                    mi = 3 - a
                    vl, vb = vg[:, a, h, :], [vg]
                    first = (c0b == 0 and a == 0)
                    last = (a == 3)
                pt = pT[pti[0] % 4]
                pti[0] += 1
                act(I("activation", out=pt[:, cols], in_=ps_[:, cols], func=AF.Exp, bias=btab[:, h, mi:mi + 1], scale=SCALE),
                    rd=[ps_, btab], wr=[pt])
                ob = obanks[hl]
                mm_group([I("matmul", ob[:, cols], lhsT=vl, rhs=pt[:, cols], start=first, stop=last)], [pt] + vb, [ob])

            inflight = []
            for i in range(len(tasks) + LA):
                if i < len(tasks):
                    inflight.append((tasks[i], emit_st(tasks[i])))
                if i >= LA:
                    task, ps_ = inflight.pop(0)
                    emit_ep(task, ps_)
                    if task[0] == "past" and task[2] == 3 and task[3] == 1:
                        load_unit()
            for hl in range(4):
                h = hh * 4 + hl
                ob = obanks[hl]
                rc = rec[hl % 2]
                dve(I("reciprocal", out=rc[0:64, :], in_=ob[64:128, :]), rd=[ob], wr=[rc])
                dve(I("tensor_tensor", out=ycT[(h % 2) * 64:(h % 2) * 64 + 64, h // 2, :],
                      in0=ob[0:64, :], in1=rc[0:64, :], op=ALU.mult), rd=[ob, rc], wr=[ycT])
            unreserve(obanks)
        if "ycT" in dumps and (t, l) == dumps["ycT"]:
            dump("ycT", ycT, ycT[:], [128, 4, T], BF16)
        if done("moba"):
            return False

        yT = [yaT, ybT, ycT]
        for fp in range(4):
            for br in range(3):
                w = next_block(l, 9 + fp * 3 + br)
                for f2 in range(2):
                    pb = bank()
                    proj_fm(w, f2 * 128, 128, xtb, pb)
                    s_ = sg[f2 * 3 + br]
                    act(I("activation", out=s_[:], in_=pb[:, :], func=AF.Sigmoid), rd=[pb], wr=[s_])
                    pbr = bank()
                    mms = [(I("matmul", pbr[:, :], lhsT=w.v[:, 8 + k, f2 * 128:(f2 + 1) * 128], rhs=yT[br][:, k, :],
                        start=(k == 0), stop=(k == 3))) for k in range(4)]
                    mm_group(mms, [w.buf, yT[br]], [pbr])
                    dve(I("tensor_tensor", out=s_[:], in0=pbr[:, :], in1=s_[:], op=ALU.mult),
                        rd=[pbr, s_], wr=[s_])
                release(w)
            for f2 in range(2):
                fc = fp * 2 + f2
                a_, b_, c_ = sg[f2 * 3], sg[f2 * 3 + 1], sg[f2 * 3 + 2]
                pool(I("tensor_tensor", out=a_[:], in0=a_[:], in1=b_[:], op=ALU.add),
                     rd=[a_, b_], wr=[a_])
                pool(I("tensor_tensor", out=mergedT[:, fc, :], in0=a_[:], in1=c_[:], op=ALU.add),
                     rd=[a_, c_], wr=[mergedT])
        if "mergedT" in dumps and (t, l) == dumps["mergedT"]:
            dump("mergedT", mergedT, mergedT[:], [128, 8, T], BF16)

        if done("merge"):
            return False
        ln_load_params(l, "ln1_g", "ln1_b")
        wo = [next_block(l, 21), next_block(l, 22)]
        for s in range(4):
            for hf in range(2):
                w = wo[hf]
                pb = bank()
                mms = [(I("matmul", pb[:, :], lhsT=mergedT[:, k, s * 128:(s + 1) * 128], rhs=w.v[:, k, :],
                          start=(k == 0), stop=(k == 7))) for k in range(8)]
                mm_group(mms, [w.buf, mergedT], [pb])
                dve(I("scalar_tensor_tensor", out=xr[s][:, hf * 512:(hf + 1) * 512], in0=xr[s][:, hf * 512:(hf + 1) * 512], scalar=ALPHA,
                      in1=pb[:, :], op0=ALU.mult, op1=ALU.add), rd=[pb, xr[s]], wr=[xr[s]])
            ln_stats(s)
        release(wo[0])
        release(wo[1])
        if done("outp"):
            return False
        ln_finish(xtb2)
        if "x1" in dumps and (t, l) == dumps["x1"]:
            dump("x1", xr[0], xr[0][:], [128, DM], F32)
        if done("ln1"):
            return False
        if done("tr2"):
            return False

        for hp in range(11):
            w = next_block(l, 23 + hp)
            for h2 in range(2):
                hc = hp * 2 + h2
                pgt = bank()
                proj_fm(w, h2 * 128, 128, xtb2, pgt)
                s_ = sg[hc % 2]
                act(I("activation", out=s_[:], in_=pgt[:, :], func=AF.Silu), rd=[pgt], wr=[s_])
                pu = bank()
                proj_fm(w, 256 + h2 * 128, 128, xtb2, pu)
                dve(I("tensor_tensor", out=hT[:, hc, :], in0=pu[:, :], in1=s_[:], op=ALU.mult),
                    rd=[pu, s_], wr=hT_al)
            release(w)
        if done("ffn1"):
            return False
        ln_load_params(l, "ln2_g", "ln2_b")
        for hf in range(2):
            pbs = reserve(4)
            k0 = 0
            for bi3, nk in enumerate((8, 8, 6)):
                w = next_block(l, 34 + hf * 3 + bi3)
                for s in range(4):
                    mms = [(I("matmul", pbs[s][:, :], lhsT=hT[:, k0 + k, s * 128:(s + 1) * 128], rhs=w.v[:, k, :],
                                                                 start=(k0 + k == 0), stop=(k0 + k == NHC - 1))) for k in range(nk)]
                    mm_group(mms, [w.buf] + hT_al, [pbs[s]])
                release(w)
                k0 += nk
                if done("dn%d" % bi3):
                    return False
            unreserve(pbs)
            for s in range(4):
                dve(I("scalar_tensor_tensor", out=xr[s][:, hf * 512:(hf + 1) * 512], in0=xr[s][:, hf * 512:(hf + 1) * 512], scalar=ALPHA,
                    in1=pbs[s][:, :], op0=ALU.mult, op1=ALU.add), rd=[pbs[s], xr[s]], wr=[xr[s]])
                ln_stats_half(s, hf)
                if hf == 1:
                    ln_aggr(s)
        if done("ffn2"):
            return False
        ln_finish(xtb if l < L - 1 else None)
        return True

    ok = True
    for t in range(NT):
        for s4 in range(4):
            if t > 0:
                dma("pool", xr[s4][:], x_d[t * T + s4 * 128:t * T + (s4 + 1) * 128, :], [], [xr[s4]], xld_chan[s4])
        for l in range(L):
            xa, xb2 = xT[0], xT[1]
            if l == 0:
                transpose_in(l, xa)
            if "xT" in dumps and (t, l) == dumps["xT"]:
                dump("xT", xa, xa[:], [128, 8, T], BF16)
            ok = tile_layer(t, l, xa, xb2)
            if not ok:
                break
        if not ok:
            break
        for s4 in range(4):
            store("pool", out_d[t * T + s4 * 128:t * T + (s4 + 1) * 128, :], xr[s4][:], [xr[s4]], [])
    fin = [ch.last for ch in st_chans if ch.last is not None]
    P.op("pool", I("engine_nop", ), waits=fin, sig=False)

    with nc.Block() as block:
        @block.tensor
        def _(e):
            P.replay("pe", e)

        @block.scalar
        def _(e):
            P.replay("act", e)

        @block.vector
        def _(e):
            P.replay("dve", e)

        @block.gpsimd
        def _(e):
            P.replay("pool", e)

        @block.sync
        def _(e):
            P.replay("sp", e)
    print("ops:", {k: len(v) for k, v in P.q.items()})
    return nc, dump_specs


_CACHE = {}


def run(inputs, S, L, n_cores, stop_after=None, dumps=None, layer_sel=None):
    key = (S, L, stop_after, str(dumps))
    if key not in _CACHE:
        _CACHE[key] = build(S, L, stop_after=stop_after, dumps=dumps)
    nc, dump_specs = _CACHE[key]
    names = ["w_in", "attn_sinks", "conv_w", "w_branch_a", "w_branch_b", "w_branch_c", "w_out", "ln1_g", "ln1_b",
             "w_ffn_gate", "w_ffn_up", "w_ffn_down", "ln2_g", "ln2_b"]
    shared = {}
    for n in names:
        a = np.asarray(inputs[n], dtype=np.float32)
        if layer_sel is not None:
            a = a[layer_sel:layer_sel + 1]
        shared[n] = np.ascontiguousarray(a)
    x = np.asarray(inputs["x"], dtype=np.float32)
    in_maps = []
    for c in range(n_cores):
        m = dict(shared)
        m["x"] = np.ascontiguousarray(x[c])
        in_maps.append(m)
    res = run_bass_kernel_spmd(nc, in_maps, core_ids=list(range(n_cores)))
    return res, dump_specs


def kernel(**inputs):
    x = np.asarray(inputs["x"])
    B, S, _ = x.shape
    res, _ = run(inputs, S, 2, B)
    return np.stack([np.asarray(r["out"]) for r in res.results], axis=0).astype(np.float32)
```

```python
import numpy as np
import concourse.bass as bass
import concourse.mybir as mybir
from concourse.bass_utils import run_bass_kernel_spmd

F32 = mybir.dt.float32
BF16 = mybir.dt.bfloat16
AF = mybir.ActivationFunctionType
ALU = mybir.AluOpType
AX = mybir.AxisListType

DM = 1024
T = 512
FFN = 2816
NHC = 22
ALPHA = 4.0 ** 0.25
LN_EPS = 1e-5
SCALE = 0.125
NEGB = -30000.0
NEGF = -1.0e30
SLOPES = [2.0 ** (-(i + 1) / 2.0) for i in range(16)]
SEM_MAX = 30000


class Ev:
    __slots__ = ("sem", "val", "key")

    def __init__(self, sem, val, key):
        self.sem = sem
        self.val = val
        self.key = key


class Chan:
    def __init__(self, P, name, step):
        self.P = P
        self.name = name
        self.step = step
        self.sem = None
        self.cnt = 0
        self.gen = 0
        self.last = None

    def next(self):
        if self.sem is None or self.cnt + self.step > SEM_MAX:
            self.sem = self.P.nc.alloc_semaphore("%s_%d" % (self.name, self.gen))
            self.key = "%s_%d" % (self.name, self.gen)
            self.gen += 1
            self.cnt = 0
        self.cnt += self.step
        self.last = Ev(self.sem, self.cnt, self.key)
        return self.last


class Buf:
    def __init__(self, name, t=None):
        self.name = name
        self.t = t
        self.w = {}
        self.r = {}

    def __getitem__(self, k):
        return self.t[k]


def _merge(d, ev):
    if ev is None:
        return
    o = d.get(ev.key)
    if o is None or o.val < ev.val:
        d[ev.key] = ev


def I(meth, *args, **kw):
    return lambda e: getattr(e, meth)(*args, **kw)


class Prog:
    ENGS = ("pe", "act", "dve", "pool", "sp")

    def __init__(self, nc):
        self.nc = nc
        self.q = {e: [] for e in self.ENGS}
        self.seen = {e: {} for e in self.ENGS}
        self.chan = {e: Chan(self, "c_" + e, 1) for e in ("pe", "act", "dve", "pool")}
        self.nops = 0

    def op(self, eng, fn, rd=(), wr=(), waits=(), sig=True, chan=None, skip_self=False):
        need = {}
        for b in rd:
            for ev in b.w.values():
                _merge(need, ev)
        for b in wr:
            for ev in b.w.values():
                _merge(need, ev)
            for ev in b.r.values():
                _merge(need, ev)
        for ev in waits:
            _merge(need, ev)
        ws = []
        seen = self.seen[eng]
        selfkey = self.chan[eng].key if (eng in self.chan and self.chan[eng].sem is not None) else None
        for key, ev in need.items():
            if key == selfkey:
                if skip_self or ev.val <= self.chan[eng].cnt - 3:
                    continue
            if seen.get(key, 0) >= ev.val:
                continue
            seen[key] = ev.val
            ws.append(ev)
        evo = None
        step = 0
        if sig:
            ch = chan if chan is not None else self.chan[eng]
            evo = ch.next()
            step = ch.step
            for b in rd:
                _merge(b.r, evo)
            for b in wr:
                _merge(b.w, evo)
        self.q[eng].append((fn, ws, evo, step))
        self.nops += 1
        return evo

    def replay(self, eng, e):
        embed_ok = eng in ("pe", "act", "dve")
        for fn, ws, evo, step in self.q[eng]:
            emb = None
            if embed_ok and ws:
                emb = ws[-1]
                ws = ws[:-1]
            for w in ws:
                e.wait_ge(w.sem, w.val)
            ins = fn(e)
            if emb is not None:
                ins._wait_ge(emb.sem, emb.val)
            if evo is not None:
                ins.then_inc(evo.sem, step)


def layer_blocks():
    B = []

    def win(c0, n):
        return (8, n, [("w_in", 0, c0, n, 0, 0, 8)])

    B.append(win(0, 512))
    B.append(win(512, 256))
    for c in range(4):
        B.append((8, 384, [("w_in", 0, 768 + c * 128, 128, 0, 0, 8),
                           ("w_in", 0, 1280 + c * 128, 128, 0, 128, 8),
                           ("w_in", 0, 1792 + c * 128, 128, 0, 256, 8)]))
    B.append(win(2816, 512))
    B.append(win(2304, 512))
    B.append(win(3328, 512))
    for fp in range(4):
        for br, nm in enumerate(("w_branch_a", "w_branch_b", "w_branch_c")):
            B.append((12, 256, [("w_in", 0, 3840 + br * 1024 + fp * 256, 256, 0, 0, 8),
                                (nm, 0, fp * 256, 256, 8, 0, 4)]))
    for hf in range(2):
        B.append((8, 512, [("w_out", 0, hf * 512, 512, 0, 0, 8)]))
    for hp in range(11):
        B.append((8, 512, [("w_ffn_gate", 0, hp * 256, 256, 0, 0, 8),
                           ("w_ffn_up", 0, hp * 256, 256, 0, 256, 8)]))
    for hf in range(2):
        for (r0, nk) in ((0, 8), (8, 8), (16, 6)):
            B.append((nk, 512, [("w_ffn_down", r0 * 128, hf * 512, 512, 0, 0, nk)]))
    return B


def build(S, L, stop_after=None, dumps=None):
    nc = bass.Bass("TRN2", target_bir_lowering=False)
    P = Prog(nc)
    NT = S // T
    NBK = S // 256
    dumps = dumps if dumps is not None else {}

    def din(name, shape):
        return nc.dram_tensor(name, shape, F32, kind="ExternalInput").ap()

    x_d = din("x", [S, DM])
    W = {
        "w_in": din("w_in", [L, DM, 6912]),
        "w_branch_a": din("w_branch_a", [L, 512, DM]),
        "w_branch_b": din("w_branch_b", [L, 512, DM]),
        "w_branch_c": din("w_branch_c", [L, 512, DM]),
        "w_out": din("w_out", [L, DM, DM]),
        "w_ffn_gate": din("w_ffn_gate", [L, DM, FFN]),
        "w_ffn_up": din("w_ffn_up", [L, DM, FFN]),
        "w_ffn_down": din("w_ffn_down", [L, FFN, DM]),
    }
    sinks_d = din("attn_sinks", [L, 8])
    convw_d = din("conv_w", [L, 3, 512])
    lnp = {n: din(n, [L, DM]) for n in ("ln1_g", "ln1_b", "ln2_g", "ln2_b")}
    out_d = nc.dram_tensor("out", [S, DM], F32, kind="ExternalOutput").ap()

    BLK = layer_blocks()
    NBLK = len(BLK)
    wsc = nc.dram_tensor("wsc", [L, NBLK, 128, 4096], BF16).ap()
    kcache = nc.dram_tensor("kcache", [L, NBK, 2, 96, 1024], BF16).ap()
    vcache = nc.dram_tensor("vcache", [L, NBK, 2, 128, 1024], BF16).ap()
    wsc_bufs = {(l, p_): Buf("wsc%d_%d" % (l, p_)) for l in range(L) for p_ in range(2)}

    def wsc_part(bi):
        return 0 if bi < 21 else 1
    kc_buf = [[Buf("kc%d_%d" % (l, j)) for j in range(NBK)] for l in range(L)]

    sb_off = [(nc.sbuf_base + 63) // 64 * 64]
    sb_top = nc.sbuf_top

    def alloc(name, shape, dtype, at=None):
        nb = int(np.prod(shape[1:])) * (4 if dtype == F32 else 2)
        nb = (nb + 63) // 64 * 64
        if at is None:
            off = sb_off[0]
            sb_off[0] += nb
            assert sb_off[0] <= sb_top, "SBUF overflow at %s: %d > %d" % (name, sb_off[0], sb_top)
        else:
            off = at
        t = nc.alloc_sbuf_tensor_at(name, list(shape), dtype, offset=off)
        b = Buf(name, t)
        b.off = off
        b.nb = nb
        return b

    ident = alloc("ident", [128, 128], BF16)
    tri = alloc("tri", [128, 128], BF16)
    Dq = alloc("Dq", [128, 128], F32)
    btab = alloc("btab", [128, 8, 66], F32)
    pm2 = alloc("pm2", [128, 64], F32)
    swab0 = alloc("swab0", [128, 2, 512], F32)
    swab1 = alloc("swab1", [128, 2, 512], F32)
    esink = alloc("esink", [128, L, 8], F32)
    convw = alloc("convw", [128, L, 4, 3], F32)
    epsb = alloc("epsb", [128, 1], F32)
    ohtab = alloc("ohtab", [128, 32], F32)
    kmf = alloc("kmf", [64, L, 8, 32], F32)
    kmh = alloc("kmh", [64, L, 2, 8, 32], BF16)
    uhalo = alloc("uhalo", [128, L, 4, 2], F32)
    kahalo = alloc("kahalo", [64, L, 2, 128], BF16)
    vahalo = alloc("vahalo", [128, L, 2, 128], BF16)
    xr = [alloc("xr%d" % i, [128, DM], F32) for i in range(4)]
    xbs = [alloc("xb%d" % i, [128, DM], BF16) for i in range(2)]
    xb_n = [0]
    xT = [alloc("xT%d" % i, [128, 8, T], BF16) for i in range(2)]
    lnpar = [alloc("lnpar%d" % i, [128, DM], F32) for i in range(2)]
    lnts = [alloc("lnt%d" % i, [128, DM], F32) for i in range(3)]
    lnt = lnts[0]
    lnst = alloc("lnst", [128, 4, 12], F32)
    lnmv = alloc("lnmv", [128, 4, 2], F32)
    lnrs = alloc("lnrs", [128, 4], F32)
    lnnm = alloc("lnnm", [128, 4], F32)
    ring = [alloc("ring%d" % i, [128, 4096], BF16) for i in range(3)]
    qTa = alloc("qTa", [64, 8, T], BF16)
    kTa = alloc("kTa", [64, 2, 128 + T], BF16)
    va = alloc("va", [128, 5, 2, 128], BF16)
    yaT = alloc("yaT", [128, 4, T], BF16)
    ybT = alloc("ybT", [128, 4, T], BF16)
    ycT = alloc("ycT", [128, 4, T], BF16)
    ubuf = alloc("ubuf", [128, 4, T + 2], F32)
    f32tmp = [alloc("f32tmp%d" % i, [128, T], F32) for i in range(4)]
    pT = [alloc("pT%d" % i, [128, T], BF16) for i in range(4)]
    rec = [alloc("rec%d" % i, [64, T], F32) for i in range(2)]
    gtmp = alloc("gtmp", [128, 8, 32], F32)
    ltmp = alloc("ltmp", [128, 8, 32], F32)
    m8 = alloc("m8", [128, 8, 8], F32)
    mbs = [alloc("mb%d" % i, [128, 8, 32], BF16) for i in range(4)]
    stg = [(alloc("stgk%d" % i, [96, 4, 256], BF16), alloc("stgv%d" % i, [128, 2, 4, 128], BF16)) for i in range(3)]
    u0 = sb_off[0]
    qTg = alloc("qTg", [96, 8, T], BF16)
    kTg = alloc("kTg", [96, 8, T], BF16)
    vg = alloc("vg", [128, 4, 8, 128], BF16)
    u1 = sb_off[0]
    assert u1 - u0 >= NHC * T * 2
    hT = alloc("hT", [128, NHC, T], BF16, at=u0)
    hT_al = [qTg, kTg, vg]
    mergedT = alloc("mergedT", [128, 8, T], BF16)
    sg = [alloc("sg%d" % i, [128, T], F32) for i in range(6)]
    print("SBUF used %d / %d" % (sb_off[0], sb_top))

    banks_t = [nc.alloc_psum_tensor("bank%d" % i, [128, 512], F32) for i in range(8)]
    banks = [Buf("bank%d" % i, banks_t[i]) for i in range(8)]
    bank_rr = [0]

    reserved = set()

    def bank():
        while True:
            i = bank_rr[0] % 8
            bank_rr[0] += 1
            if i not in reserved:
                return banks[i]

    def reserve(n):
        out = []
        for _ in range(n):
            b = bank()
            reserved.add(banks.index(b))
            out.append(b)
        return out

    def unreserve(bs):
        for b in bs:
            reserved.discard(banks.index(b))

    pre_chans = {(l, p_): Chan(P, "pre%d_%d" % (l, p_), 16) for l in range(L) for p_ in range(2)}
    ring_chan = [Chan(P, "ring%d" % i, 16) for i in range(3)]
    stg_chan = [Chan(P, "stg%d" % i, 16) for i in range(3)]
    ld_chan = Chan(P, "ld", 16)
    xld_chan = [Chan(P, "xld%d" % i, 16) for i in range(4)]
    par_chan = [Chan(P, "par%d" % i, 16) for i in range(2)]
    st_chans = [Chan(P, "st%d" % i, 16) for i in range(8)]
    st_rr = [0]

    def dma(eng, out_ap, in_ap, rd, wr, chan, waits=()):
        ws = list(waits)
        if chan.last is not None:
            ws.append(chan.last)
        return P.op(eng, I("dma_start", out=out_ap, in_=in_ap), rd=rd, wr=wr, waits=ws, chan=chan)

    def store(eng, out_ap, in_ap, rd, wr):
        ch = st_chans[st_rr[0] % len(st_chans)]
        st_rr[0] += 1
        return dma(eng, out_ap, in_ap, rd, wr, ch)

    dump_specs = []

    def dump(name, buf, ap, shape, dtype):
        if name not in dumps:
            return
        d = nc.dram_tensor("dbg_" + name, list(shape), dtype, kind="ExternalOutput").ap()
        store("pool", d, ap, [buf], [])
        dump_specs.append(name)

    def pool(fn, rd=(), wr=(), waits=()):
        return P.op("pool", fn, rd=rd, wr=wr, waits=waits)

    def dve(fn, rd=(), wr=(), waits=()):
        return P.op("dve", fn, rd=rd, wr=wr, waits=waits)

    def act(fn, rd=(), wr=(), waits=()):
        return P.op("act", fn, rd=rd, wr=wr, waits=waits)

    pool(I("iota", Dq[:], pattern=[[1, 128]], base=0, channel_multiplier=-1,
                          allow_small_or_imprecise_dtypes=True), wr=[Dq])
    pool(I("tensor_single_scalar", out=ident[:], in_=Dq[:], scalar=0.0, op=ALU.is_equal), rd=[Dq], wr=[ident])
    pool(I("tensor_scalar", out=tri[:], in0=Dq[:], scalar1=0.0, scalar2=NEGB, op0=ALU.is_lt, op1=ALU.mult),
         rd=[Dq], wr=[tri])
    pool(I("iota", lnt[:, 0:66], pattern=[[-128, 66]], base=384, channel_multiplier=1,
                          allow_small_or_imprecise_dtypes=True), wr=[lnt])
    for h in range(8):
        pool(I("tensor_scalar", out=btab[:, h, :], in0=lnt[:, 0:66], scalar1=SLOPES[8 + h], scalar2=None,
                                            op0=ALU.mult), rd=[lnt], wr=[btab])
    pool(I("iota", pm2[:], pattern=[[1, 64]], base=-32, channel_multiplier=0,
           allow_small_or_imprecise_dtypes=True), wr=[pm2])
    pool(I("tensor_scalar", out=pm2[:], in0=pm2[:], scalar1=0.0, scalar2=NEGF, op0=ALU.is_ge, op1=ALU.mult),
         rd=[pm2], wr=[pm2])
    for g in range(2):
        for hl in range(4):
            sl = SLOPES[g * 4 + hl]
            cs = slice(hl * 128, (hl + 1) * 128)
            pool(I("tensor_scalar", out=swab1[:, g, cs], in0=Dq[:], scalar1=0.0, scalar2=NEGF,
                                                        op0=ALU.is_lt, op1=ALU.mult), rd=[Dq], wr=[swab1])
            pool(I("tensor_scalar", out=lnt[:, 128:256], in0=Dq[:], scalar1=-sl, scalar2=None, op0=ALU.mult),
                 rd=[Dq], wr=[lnt])
            pool(I("tensor_tensor", out=swab1[:, g, cs], in0=swab1[:, g, cs], in1=lnt[:, 128:256],
                                                        op=ALU.add), rd=[lnt, swab1], wr=[swab1])
            dve(I("tensor_scalar", out=swab0[:, g, cs], in0=Dq[:], scalar1=0.0, scalar2=NEGF,
                  op0=ALU.is_ge, op1=ALU.mult), rd=[Dq], wr=[swab0])
            dve(I("tensor_scalar", out=lnts[1][:, 256:384], in0=Dq[:], scalar1=128.0, scalar2=-sl,
                  op0=ALU.add, op1=ALU.mult), rd=[Dq], wr=[lnts[1]])
            dve(I("tensor_tensor", out=swab0[:, g, cs], in0=swab0[:, g, cs], in1=lnts[1][:, 256:384],
                  op=ALU.add), rd=[lnts[1], swab0], wr=[swab0])
    pool(I("memset", epsb[:], LN_EPS), wr=[epsb])
    pool(I("memset", ohtab[:], 0.0), wr=[ohtab])
    pool(I("iota", ohtab[64:96, :], pattern=[[1, 32]], base=0, channel_multiplier=-1,
           allow_small_or_imprecise_dtypes=True), wr=[ohtab])
    pool(I("tensor_single_scalar", out=ohtab[64:96, :], in_=ohtab[64:96, :], scalar=0.0, op=ALU.is_equal),
         rd=[ohtab], wr=[ohtab])
    pool(I("memset", kTg[:], 0.0), wr=[kTg])
    pool(I("memset", kmh[:], 0.0), wr=[kmh])
    pool(I("memset", kmf[:], 0.0), wr=[kmf])
    pool(I("memset", uhalo[:], 0.0), wr=[uhalo])
    pool(I("memset", kahalo[:], 0.0), wr=[kahalo])
    pool(I("memset", vahalo[:], 0.0), wr=[vahalo])
    pool(I("memset", va[:], 1.0), wr=[va])
    pool(I("memset", vg[:], 1.0), wr=[vg])
    pool(I("memset", qTg[:], 0.0), wr=[qTg])
    dma("sp", esink[:].rearrange("p l h -> p (l h)"), sinks_d.rearrange("l h -> (l h)").partition_broadcast(128),
        [], [esink], ld_chan)
    act(I("activation", out=esink[:], in_=esink[:], func=AF.Exp), rd=[esink], wr=[esink])
    for l in range(L):
        for c in range(4):
            for k in range(3):
                dma("sp", convw[:, l, c, k:k + 1], convw_d[l, k, c * 128:(c + 1) * 128].rearrange("(p o) -> p o", o=1),
                    [], [convw], ld_chan)

    for s4 in range(4):
        dma("pool", xr[s4][:], x_d[s4 * 128:(s4 + 1) * 128, :], [], [xr[s4]], xld_chan[s4])
    def emit_prepass(l):
        for bi, (kc, ncols, parts) in enumerate(BLK):
            dst = wsc[l, bi, :, 0:kc * ncols].rearrange("p (k n) -> p k n", k=kc)
            for (src, r0, c0, n, dk, dc, nk) in parts:
                s_ap = W[src][l, r0:r0 + nk * 128, c0:c0 + n].rearrange("(k p) n -> p k n", p=128)
                d_ap = dst[:, dk:dk + nk, dc:dc + n]
                pre_ev = P.op("pool", I("dma_start", out=d_ap, in_=s_ap), chan=pre_chans[(l, wsc_part(bi))])
                _merge(wsc_bufs[(l, wsc_part(bi))].w, pre_ev)

    emit_prepass(0)

    class WB:
        pass

    ring_free = [True, True, True]
    seq = [(t, l, bi) for t in range(NT) for l in range(L) for bi in range(NBLK)]
    seq_pos = [0]
    pending = []

    def issue_one():
        free = [i for i in range(3) if ring_free[i]]
        if not free or seq_pos[0] >= len(seq):
            return False
        slot = free[0]
        ring_free[slot] = False
        _, l, bi = seq[seq_pos[0]]
        seq_pos[0] += 1
        kc, ncols, _ = BLK[bi]
        rb = ring[slot]
        dma("sp", rb[:, 0:kc * ncols], wsc[l, bi, :, 0:kc * ncols], [wsc_bufs[(l, wsc_part(bi))]], [rb], ring_chan[slot])
        w = WB()
        w.slot = slot
        w.buf = rb
        w.v = rb[:, 0:kc * ncols].rearrange("p (k n) -> p k n", k=kc)
        pending.append((l, bi, w))
        return True

    def next_block(l, bi):
        while not pending:
            assert issue_one()
        ll, bb, w = pending.pop(0)
        assert (ll, bb) == (l, bi), ((ll, bb), (l, bi))
        return w

    def release(w):
        ring_free[w.slot] = True
        while issue_one():
            pass

    def mm_group(mms, rd, wr):
        n = len(mms)
        ev = None
        for i, fn in enumerate(mms):
            ev = P.op("pe", fn, rd=(rd if i == 0 else ()), wr=(wr if i == 0 else ()), sig=(i == n - 1), skip_self=True)
        for b in rd:
            _merge(b.r, ev)
        for b in wr:
            _merge(b.w, ev)
        return ev

    def proj_fm(w, c0, ncol, xt, pb, prow=None):
        o = pb[0:ncol, :]
        mms = [(I("matmul", o, lhsT=w.v[:, k, c0:c0 + ncol], rhs=xt[:, k, :], start=(k == 0), stop=(k == 7)))
               for k in range(8)]
        return mm_group(mms, [w.buf, xt], [pb])

    def transpose_in(l, xtb):
        for s in range(4):
            xb = xbs[xb_n[0] % 2]
            xb_n[0] += 1
            act(I("activation", out=xb[:], in_=xr[s][:], func=AF.Copy), rd=[xr[s]], wr=[xb])
            pb = bank()
            pbv = pb[:].bitcast(BF16)
            mms = [(I("transpose", out=pbv[:, k * 128:(k + 1) * 128], in_=xb[:, k * 128:(k + 1) * 128],
                      identity=ident[:])) for k in range(8)]
            mm_group(mms, [xb, ident], [pb])
            dve(I("tensor_copy", out=xtb[:, :, s * 128:(s + 1) * 128],
                  in_=pbv.rearrange("p (k t) -> p k t", k=8)), rd=[pb], wr=[xtb])

    def ln_load_params(l, gname, bname):
        dma("sp", lnpar[0][:], lnp[gname][l].partition_broadcast(128), [], [lnpar[0]], par_chan[0])
        dma("sp", lnpar[1][:], lnp[bname][l].partition_broadcast(128), [], [lnpar[1]], par_chan[1])

    def ln_stats_half(s, hf):
        dve(I("bn_stats", out=lnst[:, s, hf * 6:hf * 6 + 6], in_=xr[s][:, hf * 512:(hf + 1) * 512]), rd=[xr[s]], wr=[lnst])

    def ln_aggr(s):
        dve(I("bn_aggr", out=lnmv[:, s, :], in_=lnst[:, s, :]), rd=[lnst], wr=[lnmv])

    def ln_stats(s):
        ln_stats_half(s, 0)
        ln_stats_half(s, 1)
        ln_aggr(s)

    def ln_finish(next_xt):
        act(I("activation", out=lnrs[:], in_=lnmv[:, :, 1], func=AF.Sqrt, bias=epsb[:], scale=1.0),
            rd=[lnmv, epsb], wr=[lnrs])
        dve(I("reciprocal", out=lnrs[:], in_=lnrs[:]), rd=[lnrs], wr=[lnrs])
        dve(I("scalar_tensor_tensor", out=lnnm[:], in0=lnmv[:, :, 0], scalar=-1.0, in1=lnrs[:], op0=ALU.mult, op1=ALU.mult),
            rd=[lnmv, lnrs], wr=[lnnm])
        pend_ev = None
        for s in range(4):
            lt = lnts[s % 3]
            act(I("activation", out=lt[:], in_=xr[s][:], func=AF.Identity, bias=lnnm[:, s:s + 1],
                  scale=lnrs[:, s:s + 1]), rd=[xr[s], lnnm, lnrs], wr=[lt])
            dve(I("tensor_tensor", out=lt[:], in0=lt[:], in1=lnpar[0][:], op=ALU.mult), rd=[lt, lnpar[0]], wr=[lt])
            if next_xt is not None:
                xb = xbs[xb_n[0] % 2]
                xb_n[0] += 1
                dve(I("tensor_tensor", out=xb[:], in0=lt[:], in1=lnpar[1][:], op=ALU.add), rd=[lt, lnpar[1]], wr=[xb])
            pool(I("tensor_tensor", out=xr[s][:], in0=lt[:], in1=lnpar[1][:], op=ALU.add),
                 rd=[lt, lnpar[1]], wr=[xr[s]])
            if next_xt is not None:
                pb = bank()
                pbv = pb[:].bitcast(BF16)
                mms = [(I("transpose", out=pbv[:, k * 128:(k + 1) * 128], in_=xb[:, k * 128:(k + 1) * 128],
                          identity=ident[:])) for k in range(8)]
                mm_group(mms, [xb, ident], [pb])
                if pend_ev is not None:
                    pend_ev()
                pend_ev = (lambda pb=pb, pbv=pbv, s=s: act(I("activation", out=next_xt[:, :, s * 128:(s + 1) * 128],
                           in_=pbv.rearrange("p (k t) -> p k t", k=8), func=AF.Copy), rd=[pb], wr=[next_xt]))
        if pend_ev is not None:
            pend_ev()

    def tile_layer(t, l, xtb, xtb2):
        c0b = 2 * t
        stage = [0]

        def done(name):
            return stop_after is not None and stop_after == (t, l, name)

        if done("init"):
            return False
        LA = 2
        pti = [0]
        units = [(hh_, j_) for hh_ in range(2) for j_ in range(c0b)]
        unit_slot = {}
        next_unit = [0]

        def load_unit():
            if next_unit[0] >= len(units):
                return
            hh_, j_ = units[next_unit[0]]
            next_unit[0] += 1
            slot = stage[0] % 3
            stage[0] += 1
            sk, sv = stg[slot]
            dma("sp", sk[:].rearrange("p h k -> p (h k)"), kcache[l, j_, hh_], [kc_buf[l][j_]], [sk], stg_chan[slot])
            dma("sp", sv[:].rearrange("p c h d -> p (c h d)"), vcache[l, j_, hh_], [kc_buf[l][j_]], [sv], stg_chan[slot])
            unit_slot[(hh_, j_)] = (sk, sv)

        for _ in range(3):
            load_unit()
        w = next_block(l, 0)
        for hp in range(4):
            pb = bank()
            proj_fm(w, hp * 128, 128, xtb, pb)
            act(I("activation", out=qTa[:, 2 * hp, :], in_=pb[0:64, :], func=AF.Copy), rd=[pb], wr=[qTa])
            act(I("activation", out=qTa[:, 2 * hp + 1, :], in_=pb[64:128, :], func=AF.Copy), rd=[pb], wr=[qTa])
        release(w)
        w = next_block(l, 1)
        pool(I("tensor_copy", out=kTa[:, :, 0:128], in_=kahalo[:, l, :, :]), rd=[kahalo], wr=[kTa])
        pool(I("tensor_copy", out=va[:, 0, :, 0:64], in_=vahalo[:, l, :, 0:64]), rd=[vahalo], wr=[va])
        pb = bank()
        proj_fm(w, 0, 128, xtb, pb)
        for g in range(2):
            act(I("activation", out=kTa[:, g, 128:128 + T], in_=pb[g * 64:(g + 1) * 64, :], func=AF.Copy), rd=[pb], wr=[kTa])
        pb = bank()
        for s in range(4):
            o = pb[:, s * 128:(s + 1) * 128]
            mms = [(I("matmul", o, lhsT=xtb[:, k, s * 128:(s + 1) * 128], rhs=w.v[:, k, 128:256],
                                                       start=(k == 0), stop=(k == 7))) for k in range(8)]
            mm_group(mms, [w.buf, xtb], [pb])
        dve(I("tensor_copy", out=va[:, 1:5, :, 0:64],
                                           in_=pb[:, :].rearrange("p (s g d) -> p s g d", s=4, g=2)), rd=[pb], wr=[va])
        pool(I("tensor_copy", out=kahalo[:, l, :, :], in_=kTa[:, :, T:T + 128]), rd=[kTa], wr=[kahalo])
        pool(I("tensor_copy", out=vahalo[:, l, :, 0:64], in_=va[:, 4, :, 0:64]), rd=[va], wr=[vahalo])
        release(w)
        if done("proj"):
            return False
        if "qTa" in dumps and (t, l) == dumps["qTa"]:
            dump("qTa", qTa, qTa[:], [64, 8, T], BF16)
            dump("kTa", kTa, kTa[:], [64, 2, 128 + T], BF16)
            dump("va", va, va[:], [128, 5, 2, 128], BF16)

        def fill_gen():
            dve(I("tensor_copy", out=ubuf[:, :, 0:2], in_=uhalo[:, l, :, :]), rd=[uhalo], wr=[ubuf])
            for c in range(4):
                w = next_block(l, 2 + c)
                ph = bank()
                proj_fm(w, 0, 128, xtb, ph)
                hs = f32tmp[0]
                act(I("activation", out=hs[:], in_=ph[:, :], func=AF.Copy), rd=[ph], wr=[hs])
                pc_ = bank()
                proj_fm(w, 256, 128, xtb, pc_)
                dve(I("tensor_tensor", out=ubuf[:, c, 2:T + 2], in0=pc_[:, :], in1=hs[:], op=ALU.mult),
                    rd=[pc_, hs], wr=[ubuf])
                acc = f32tmp[1]
                dve(I("tensor_scalar", out=acc[:], in0=ubuf[:, c, 0:T], scalar1=convw[:, l, c, 0:1],
                                                            scalar2=None, op0=ALU.mult), rd=[ubuf, convw], wr=[acc])
                dve(I("scalar_tensor_tensor", out=acc[:], in0=ubuf[:, c, 1:T + 1], scalar=convw[:, l, c, 1:2],
                                                                   in1=acc[:], op0=ALU.mult, op1=ALU.add), rd=[ubuf, convw, acc], wr=[acc])
                dve(I("scalar_tensor_tensor", out=acc[:], in0=ubuf[:, c, 2:T + 2], scalar=convw[:, l, c, 2:3],
                                                                   in1=acc[:], op0=ALU.mult, op1=ALU.add), rd=[ubuf, convw, acc], wr=[acc])
                pB = bank()
                proj_fm(w, 128, 128, xtb, pB)
                release(w)
                dve(I("tensor_tensor", out=ybT[:, c, :], in0=pB[:, :], in1=acc[:], op=ALU.mult),
                    rd=[pB, acc], wr=[ybT])
                yield
            dve(I("tensor_copy", out=uhalo[:, l, :, :], in_=ubuf[:, :, T:T + 2]), rd=[ubuf], wr=[uhalo])

            w = next_block(l, 6)
            for hp in range(4):
                pb = bank()
                proj_fm(w, hp * 128, 128, xtb, pb)
                act(I("activation", out=kTg[0:64, 2 * hp, :], in_=pb[0:64, :], func=AF.Copy), rd=[pb], wr=[kTg])
                eva = act(I("activation", out=kTg[0:64, 2 * hp + 1, :], in_=pb[64:128, :], func=AF.Copy), rd=[pb], wr=[kTg])
                for a2 in range(2):
                    dve(I("tensor_reduce", out=kmf[:, l, 2 * hp + a2, c0b:c0b + 2],
                          in_=pb[a2 * 64:(a2 + 1) * 64, :].rearrange("p (b k) -> p b k", b=2),
                          axis=AX.X, op=ALU.add), rd=[pb], wr=[kmf], waits=[eva])
                if hp == 1:
                    yield
            release(w)
            dve(I("tensor_scalar", out=kmf[:, l, :, c0b:c0b + 2], in0=kmf[:, l, :, c0b:c0b + 2], scalar1=1.0 / 256.0,
                                          scalar2=None, op0=ALU.mult), rd=[kmf], wr=[kmf])
            dve(I("tensor_copy", out=kmh[:, l, 0, :, c0b:c0b + 2], in_=kmf[:, l, :, c0b:c0b + 2]), rd=[kmf], wr=[kmh])
            dve(I("tensor_tensor", out=kmh[:, l, 1, :, c0b:c0b + 2], in0=kmf[:, l, :, c0b:c0b + 2],
                                          in1=kmh[:, l, 0, :, c0b:c0b + 2], op=ALU.subtract), rd=[kmf, kmh], wr=[kmh])
            for blk in range(2):
                dve(I("tensor_scalar", out=kTg[64:96, :, blk * 256:(blk + 1) * 256].rearrange("p (c a) k -> p c a k", a=2),
                      in0=ubuf[64:96, :, 0:512].rearrange("p c (a k) -> p c a k", a=2),
                      scalar1=0.0, scalar2=ohtab[64:96, c0b + blk:c0b + blk + 1], op0=ALU.mult, op1=ALU.add),
                    rd=[ubuf, ohtab], wr=[kTg])
            w = next_block(l, 7)
            for hp in range(4):
                pb = bank()
                proj_fm(w, hp * 128, 128, xtb, pb)
                act(I("activation", out=qTg[0:64, 2 * hp, :], in_=pb[0:64, :], func=AF.Copy), rd=[pb], wr=[qTg])
                act(I("activation", out=qTg[0:64, 2 * hp + 1, :], in_=pb[64:128, :], func=AF.Copy), rd=[pb], wr=[qTg])
                if hp == 1:
                    yield
            release(w)
            yield
            pgs = []
            for s in range(4):
                pg = bank()
                mms = []
                for h in range(8):
                    for hi in range(2):
                        mms.append(I("matmul", pg[:, h * 32:(h + 1) * 32], lhsT=qTg[0:64, h, s * 128:(s + 1) * 128],
                                     rhs=kmh[:, l, hi, h, :], start=(hi == 0), stop=(hi == 1)))
                mm_group(mms, [qTg, kmh], [pg])
                pgs.append(pg)
            for s in range(4):
                c = c0b + (s // 2)
                pg = pgs[s]
                dve(I("tensor_tensor", out=gtmp[:], in0=pg[:, 0:256].rearrange("p (h n) -> p h n", h=8),
                      in1=pm2[:, 32 - c:64 - c].rearrange("p (o n) -> p o n", o=1).to_broadcast([128, 8, 32]), op=ALU.add),
                    rd=[pg, pm2], wr=[gtmp])
                for h in range(8):
                    dve(I("max", out=m8[:, h, :], in_=gtmp[:, h, :]), rd=[gtmp], wr=[m8])
                dve(I("tensor_tensor", out=ltmp[:], in0=gtmp[:], in1=m8[:, :, 2:3].to_broadcast([128, 8, 32]), op=ALU.is_lt),
                    rd=[gtmp, m8], wr=[ltmp])
                dve(I("tensor_scalar", out=mbs[s][:], in0=ltmp[:], scalar1=NEGB, scalar2=None, op0=ALU.mult), rd=[ltmp], wr=[mbs[s]])
            pool(I("memset", vg[:, :, :, 64:128], 1.0), wr=[vg])
            w = next_block(l, 8)
            for s in range(4):
                pb = bank()
                o = pb[:, :]
                mms = [(I("matmul", o, lhsT=xtb[:, k, s * 128:(s + 1) * 128], rhs=w.v[:, k, :],
                                                           start=(k == 0), stop=(k == 7))) for k in range(8)]
                mm_group(mms, [w.buf, xtb], [pb])
                dve(I("tensor_copy", out=vg[:, s, :, 0:64], in_=pb[:, :].rearrange("p (h d) -> p h d", h=8)),
                    rd=[pb], wr=[vg])
            release(w)
            for s in range(4):
                pt_ = bank()
                ptv = pt_[:].bitcast(BF16)
                mms = [(I("transpose", out=ptv[64:96, h * 128:(h + 1) * 128], in_=mbs[s][:, h, :], identity=ident[:]))
                       for h in range(8)]
                mm_group(mms, [mbs[s], ident], [pt_])
                dve(I("tensor_copy", out=qTg[64:96, :, s * 128:(s + 1) * 128],
                      in_=ptv[64:96, :].rearrange("p (h q) -> p h q", h=8)), rd=[pt_], wr=[qTg])
            if t < NT - 1:
                for blk in range(2):
                    for hh in range(2):
                        store("pool", kcache[l, c0b + blk, hh].rearrange("p (h k) -> p h k", h=4),
                              kTg[:, hh * 4:(hh + 1) * 4, blk * 256:(blk + 1) * 256], [kTg], [kc_buf[l][c0b + blk]])
                        store("pool", vcache[l, c0b + blk, hh].rearrange("p (c h d) -> p c h d", c=2, h=4),
                              vg[:, 2 * blk:2 * blk + 2, hh * 4:(hh + 1) * 4, :], [vg], [kc_buf[l][c0b + blk]])

            yield

        def swa_stage1(b, g):
            gb = 4 * t + b
            chunks = ([0] if gb > 0 else []) + [1]
            pTs = []
            for pc in chunks:
                pb = bank()
                kcols = slice(b * 128 + (0 if pc == 0 else 128), b * 128 + (128 if pc == 0 else 256))
                mms = []
                for hl in range(4):
                    h = g * 4 + hl
                    mms.append(I("matmul", pb[:, hl * 128:(hl + 1) * 128], lhsT=kTa[:, g, kcols],
                                 rhs=qTa[:, h, b * 128:(b + 1) * 128], start=True, stop=True))
                mm_group(mms, [kTa, qTa], [pb])
                ft = f32tmp[(b * 4 + g * 2 + pc) % 4]
                sw_ = swab0 if pc == 0 else swab1
                dve(I("scalar_tensor_tensor", out=ft[:], in0=pb[:, :], scalar=SCALE, in1=sw_[:, g, :],
                      op0=ALU.mult, op1=ALU.add), rd=[pb, sw_], wr=[ft])
                pt = pT[(b * 4 + g * 2 + pc) % 4]
                act(I("activation", out=pt[:], in_=ft[:], func=AF.Exp), rd=[ft], wr=[pt])
                pTs.append((pc, pt))
            return pTs

        def swa_stage2(b, g, pTs):
            po = bank()
            mms = []
            for i, (pc, pt) in enumerate(pTs):
                vblk = b + pc
                mms.append(I("matmul", po[:, :], lhsT=va[:, vblk, g, :], rhs=pt[:], start=(i == 0), stop=(i == len(pTs) - 1)))
            mm_group(mms, [va] + [p_[1] for p_ in pTs], [po])
            rc = rec[(b * 2 + g) % 2]
            for hl in range(4):
                h = g * 4 + hl
                dve(I("tensor_scalar", out=rc[0:64, hl * 128:(hl + 1) * 128], in0=po[64:128, hl * 128:(hl + 1) * 128],
                      scalar1=esink[64:128, l, h:h + 1], scalar2=None, op0=ALU.add), rd=[po, esink], wr=[rc])
            dve(I("reciprocal", out=rc[0:64, :], in_=rc[0:64, :]), rd=[rc], wr=[rc])
            for par in range(2):
                dve(I("tensor_tensor", out=yaT[par * 64:(par + 1) * 64, 2 * g:2 * g + 2, b * 128:(b + 1) * 128],
                      in0=po[0:64, :].rearrange("p (c a q) -> p c a q", c=2, a=2)[:, :, par, :],
                      in1=rc[0:64, :].rearrange("p (c a q) -> p c a q", c=2, a=2)[:, :, par, :],
                      op=ALU.mult), rd=[po, rc], wr=[yaT])

        filler = fill_gen()
        pend = None
        for b in range(4):
            for g in range(2):
                r_ = swa_stage1(b, g)
                next(filler, None)
                if pend is not None:
                    swa_stage2(*pend)
                pend = (b, g, r_)
        next(filler, None)
        swa_stage2(*pend)
        for _ in filler:
            pass
        if "yaT" in dumps and (t, l) == dumps["yaT"]:
            dump("yaT", yaT, yaT[:], [128, 4, T], BF16)
        if done("swa"):
            return False
        if "qTg" in dumps and (t, l) == dumps["qTg"]:
            dump("qTg", qTg, qTg[:], [96, 8, T], BF16)
            dump("kTg", kTg, kTg[:], [96, 8, T], BF16)
            dump("vg", vg, vg[:], [128, 4, 8, 128], BF16)
            dump("gtmp", gtmp, gtmp[:], [128, 8, 32], F32)

        if done("mgate"):
            return False
        if t == 0 and l == 0:
            for l2 in range(1, L):
                emit_prepass(l2)
        for hh in range(2):
            obanks = reserve(4)
            tasks = [("past", j, hl, ch) for j in range(c0b) for hl in range(4) for ch in range(2)]
            tasks += [("own", sub, hl, 0) for hl in range(4) for sub in range(4)]

            def emit_st(task):
                kind, a, hl, ch = task
                h = hh * 4 + hl
                ps_ = bank()
                if kind == "past":
                    sk, sv = unit_slot[(hh, a)]
                    mm_group([I("matmul", ps_[:, :], lhsT=sk[:, hl, ch * 128:(ch + 1) * 128], rhs=qTg[:, h, :], start=True, stop=True)],
                             [sk, qTg], [ps_])
                elif a == 0:
                    mm_group([
                        I("matmul", ps_[:, 0:256], lhsT=kTg[0:64, h, 0:128], rhs=qTg[0:64, h, 0:256], start=True, stop=False),
                        I("matmul", ps_[:, 0:128], lhsT=ident[:], rhs=tri[:], start=False, stop=False),
                        I("matmul", ps_[:, 256:512], lhsT=kTg[:, h, 0:128], rhs=qTg[:, h, 256:512], start=False, stop=True),
                    ], [kTg, qTg, ident, tri], [ps_])
                elif a == 1:
                    mm_group([
                        I("matmul", ps_[:, 128:256], lhsT=kTg[0:64, h, 128:256], rhs=qTg[0:64, h, 128:256], start=True, stop=False),
                        I("matmul", ps_[:, 128:256], lhsT=ident[:], rhs=tri[:], start=False, stop=False),
                        I("matmul", ps_[:, 256:512], lhsT=kTg[:, h, 128:256], rhs=qTg[:, h, 256:512], start=False, stop=True),
                    ], [kTg, qTg, ident, tri], [ps_])
                elif a == 2:
                    mm_group([
                        I("matmul", ps_[:, 256:512], lhsT=kTg[0:64, h, 256:384], rhs=qTg[0:64, h, 256:512], start=True, stop=False),
                        I("matmul", ps_[:, 256:384], lhsT=ident[:], rhs=tri[:], start=False, stop=True),
                    ], [kTg, qTg, ident, tri], [ps_])
                else:
                    mm_group([
                        I("matmul", ps_[:, 384:512], lhsT=kTg[0:64, h, 384:512], rhs=qTg[0:64, h, 384:512], start=True, stop=False),
                        I("matmul", ps_[:, 384:512], lhsT=ident[:], rhs=tri[:], start=False, stop=True),
                    ], [kTg, qTg, ident, tri], [ps_])
                return ps_

            def emit_ep(task, ps_):
                kind, a, hl, ch = task
                h = hh * 4 + hl
                if kind == "past":
                    sk, sv = unit_slot[(hh, a)]
                    cols = slice(0, T)
                    mi = (4 * t - 2 * a - ch) + 3
                    vl, vb = sv[:, ch, hl, :], [sv]
                    first = (a == 0 and ch == 0)
                    last = False
                else:
                    cols = (slice(0, 512), slice(128, 512), slice(256, 512), slice(384, 512))[a]
                    mi = 3 - a
                    vl, vb = vg[:, a, h, :], [vg]
                    first = (c0b == 0 and a == 0)
                    last = (a == 3)
                pt = pT[pti[0] % 4]
                pti[0] += 1
                act(I("activation", out=pt[:, cols], in_=ps_[:, cols], func=AF.Exp, bias=btab[:, h, mi:mi + 1], scale=SCALE),
                    rd=[ps_, btab], wr=[pt])
                ob = obanks[hl]
                mm_group([I("matmul", ob[:, cols], lhsT=vl, rhs=pt[:, cols], start=first, stop=last)], [pt] + vb, [ob])

            inflight = []
            for i in range(len(tasks) + LA):
                if i < len(tasks):
                    inflight.append((tasks[i], emit_st(tasks[i])))
                if i >= LA:
                    task, ps_ = inflight.pop(0)
                    emit_ep(task, ps_)
                    if task[0] == "past" and task[2] == 3 and task[3] == 1:
                        load_unit()
            for hl in range(4):
                h = hh * 4 + hl
                ob = obanks[hl]
                rc = rec[hl % 2]
                dve(I("reciprocal", out=rc[0:64, :], in_=ob[64:128, :]), rd=[ob], wr=[rc])
                dve(I("tensor_tensor", out=ycT[(h % 2) * 64:(h % 2) * 64 + 64, h // 2, :],
                      in0=ob[0:64, :], in1=rc[0:64, :], op=ALU.mult), rd=[ob, rc], wr=[ycT])
            unreserve(obanks)
        if "ycT" in dumps and (t, l) == dumps["ycT"]:
            dump("ycT", ycT, ycT[:], [128, 4, T], BF16)
        if done("moba"):
            return False

        yT = [yaT, ybT, ycT]
        for fp in range(4):
            for br in range(3):
                w = next_block(l, 9 + fp * 3 + br)
                for f2 in range(2):
                    pb = bank()
                    proj_fm(w, f2 * 128, 128, xtb, pb)
                    s_ = sg[f2 * 3 + br]
                    act(I("activation", out=s_[:], in_=pb[:, :], func=AF.Sigmoid), rd=[pb], wr=[s_])
                    pbr = bank()
                    mms = [(I("matmul", pbr[:, :], lhsT=w.v[:, 8 + k, f2 * 128:(f2 + 1) * 128], rhs=yT[br][:, k, :],
                        start=(k == 0), stop=(k == 3))) for k in range(4)]
                    mm_group(mms, [w.buf, yT[br]], [pbr])
                    dve(I("tensor_tensor", out=s_[:], in0=pbr[:, :], in1=s_[:], op=ALU.mult),
                        rd=[pbr, s_], wr=[s_])
                release(w)
            for f2 in range(2):
                fc = fp * 2 + f2
                a_, b_, c_ = sg[f2 * 3], sg[f2 * 3 + 1], sg[f2 * 3 + 2]
                pool(I("tensor_tensor", out=a_[:], in0=a_[:], in1=b_[:], op=ALU.add),
                     rd=[a_, b_], wr=[a_])
                pool(I("tensor_tensor", out=mergedT[:, fc, :], in0=a_[:], in1=c_[:], op=ALU.add),
                     rd=[a_, c_], wr=[mergedT])
        if "mergedT" in dumps and (t, l) == dumps["mergedT"]:
            dump("mergedT", mergedT, mergedT[:], [128, 8, T], BF16)

        if done("merge"):
            return False
        ln_load_params(l, "ln1_g", "ln1_b")
        wo = [next_block(l, 21), next_block(l, 22)]
        for s in range(4):
            for hf in range(2):
                w = wo[hf]
                pb = bank()
                mms = [(I("matmul", pb[:, :], lhsT=mergedT[:, k, s * 128:(s + 1) * 128], rhs=w.v[:, k, :],
                          start=(k == 0), stop=(k == 7))) for k in range(8)]
                mm_group(mms, [w.buf, mergedT], [pb])
                dve(I("scalar_tensor_tensor", out=xr[s][:, hf * 512:(hf + 1) * 512], in0=xr[s][:, hf * 512:(hf + 1) * 512], scalar=ALPHA,
                      in1=pb[:, :], op0=ALU.mult, op1=ALU.add), rd=[pb, xr[s]], wr=[xr[s]])
            ln_stats(s)
        release(wo[0])
        release(wo[1])
        if done("outp"):
            return False
        ln_finish(xtb2)
        if "x1" in dumps and (t, l) == dumps["x1"]:
            dump("x1", xr[0], xr[0][:], [128, DM], F32)
        if done("ln1"):
            return False
        if done("tr2"):
            return False

        for hp in range(11):
            w = next_block(l, 23 + hp)
            for h2 in range(2):
                hc = hp * 2 + h2
                pgt = bank()
                proj_fm(w, h2 * 128, 128, xtb2, pgt)
                s_ = sg[hc % 2]
                act(I("activation", out=s_[:], in_=pgt[:, :], func=AF.Silu), rd=[pgt], wr=[s_])
                pu = bank()
                proj_fm(w, 256 + h2 * 128, 128, xtb2, pu)
                dve(I("tensor_tensor", out=hT[:, hc, :], in0=pu[:, :], in1=s_[:], op=ALU.mult),
                    rd=[pu, s_], wr=hT_al)
            release(w)
        if done("ffn1"):
            return False
        ln_load_params(l, "ln2_g", "ln2_b")
        for hf in range(2):
            pbs = reserve(4)
            k0 = 0
            for bi3, nk in enumerate((8, 8, 6)):
                w = next_block(l, 34 + hf * 3 + bi3)
                for s in range(4):
                    mms = [(I("matmul", pbs[s][:, :], lhsT=hT[:, k0 + k, s * 128:(s + 1) * 128], rhs=w.v[:, k, :],
                                                                 start=(k0 + k == 0), stop=(k0 + k == NHC - 1))) for k in range(nk)]
                    mm_group(mms, [w.buf] + hT_al, [pbs[s]])
                release(w)
                k0 += nk
                if done("dn%d" % bi3):
                    return False
            unreserve(pbs)
            for s in range(4):
                dve(I("scalar_tensor_tensor", out=xr[s][:, hf * 512:(hf + 1) * 512], in0=xr[s][:, hf * 512:(hf + 1) * 512], scalar=ALPHA,
                    in1=pbs[s][:, :], op0=ALU.mult, op1=ALU.add), rd=[pbs[s], xr[s]], wr=[xr[s]])
                ln_stats_half(s, hf)
                if hf == 1:
                    ln_aggr(s)
        if done("ffn2"):
            return False
        ln_finish(xtb if l < L - 1 else None)
        return True

    ok = True
    for t in range(NT):
        for s4 in range(4):
            if t > 0:
                dma("pool", xr[s4][:], x_d[t * T + s4 * 128:t * T + (s4 + 1) * 128, :], [], [xr[s4]], xld_chan[s4])
        for l in range(L):
            xa, xb2 = xT[0], xT[1]
            if l == 0:
                transpose_in(l, xa)
            if "xT" in dumps and (t, l) == dumps["xT"]:
                dump("xT", xa, xa[:], [128, 8, T], BF16)
            ok = tile_layer(t, l, xa, xb2)
            if not ok:
                break
        if not ok:
            break
        for s4 in range(4):
            store("pool", out_d[t * T + s4 * 128:t * T + (s4 + 1) * 128, :], xr[s4][:], [xr[s4]], [])
    fin = [ch.last for ch in st_chans if ch.last is not None]
    P.op("pool", I("engine_nop", ), waits=fin, sig=False)

    with nc.Block() as block:
        @block.tensor
        def _(e):
            P.replay("pe", e)

        @block.scalar
        def _(e):
            P.replay("act", e)

        @block.vector
        def _(e):
            P.replay("dve", e)

        @block.gpsimd
        def _(e):
            P.replay("pool", e)

        @block.sync
        def _(e):
            P.replay("sp", e)
    print("ops:", {k: len(v) for k, v in P.q.items()})
    return nc, dump_specs


_CACHE = {}


def run(inputs, S, L, n_cores, stop_after=None, dumps=None, layer_sel=None):
    key = (S, L, stop_after, str(dumps))
    if key not in _CACHE:
        _CACHE[key] = build(S, L, stop_after=stop_after, dumps=dumps)
    nc, dump_specs = _CACHE[key]
    names = ["w_in", "attn_sinks", "conv_w", "w_branch_a", "w_branch_b", "w_branch_c", "w_out", "ln1_g", "ln1_b",
             "w_ffn_gate", "w_ffn_up", "w_ffn_down", "ln2_g", "ln2_b"]
    shared = {}
    for n in names:
        a = np.asarray(inputs[n], dtype=np.float32)
        if layer_sel is not None:
            a = a[layer_sel:layer_sel + 1]
        shared[n] = np.ascontiguousarray(a)
    x = np.asarray(inputs["x"], dtype=np.float32)
    in_maps = []
    for c in range(n_cores):
        m = dict(shared)
        m["x"] = np.ascontiguousarray(x[c])
        in_maps.append(m)
    res = run_bass_kernel_spmd(nc, in_maps, core_ids=list(range(n_cores)))
    return res, dump_specs


def kernel(**inputs):
    x = np.asarray(inputs["x"])
    B, S, _ = x.shape
    res, _ = run(inputs, S, 2, B)
    return np.stack([np.asarray(r["out"]) for r in res.results], axis=0).astype(np.float32)
```

```python
import numpy as np
import concourse.bass as bass
import concourse.mybir as mybir
from concourse.bass_utils import run_bass_kernel_spmd

F32 = mybir.dt.float32
BF16 = mybir.dt.bfloat16
AF = mybir.ActivationFunctionType
ALU = mybir.AluOpType
AX = mybir.AxisListType

DM = 1024
T = 512
FFN = 2816
NHC = 22
ALPHA = 4.0 ** 0.25
LN_EPS = 1e-5
SCALE = 0.125
NEGB = -30000.0
NEGF = -1.0e30
SLOPES = [2.0 ** (-(i + 1) / 2.0) for i in range(16)]
SEM_MAX = 30000


class Ev:
    __slots__ = ("sem", "val", "key")

    def __init__(self, sem, val, key):
        self.sem = sem
        self.val = val
        self.key = key


class Chan:
    def __init__(self, P, name, step):
        self.P = P
        self.name = name
        self.step = step
        self.sem = None
        self.cnt = 0
        self.gen = 0
        self.last = None

    def next(self):
        if self.sem is None or self.cnt + self.step > SEM_MAX:
            self.sem = self.P.nc.alloc_semaphore("%s_%d" % (self.name, self.gen))
            self.key = "%s_%d" % (self.name, self.gen)
            self.gen += 1
            self.cnt = 0
        self.cnt += self.step
        self.last = Ev(self.sem, self.cnt, self.key)
        return self.last


class Buf:
    def __init__(self, name, t=None):
        self.name = name
        self.t = t
        self.w = {}
        self.r = {}

    def __getitem__(self, k):
        return self.t[k]


def _merge(d, ev):
    if ev is None:
        return
    o = d.get(ev.key)
    if o is None or o.val < ev.val:
        d[ev.key] = ev


def I(meth, *args, **kw):
    f = lambda e: getattr(e, meth)(*args, **kw)
    f.meth = meth
    return f


class Prog:
    ENGS = ("pe", "act", "dve", "pool", "sp")

    def __init__(self, nc):
        self.nc = nc
        self.q = {e: [] for e in self.ENGS}
        self.seen = {e: {} for e in self.ENGS}
        self.chan = {e: Chan(self, "c_" + e, 1) for e in ("pe", "act", "dve", "pool")}
        self.nops = 0

    def op(self, eng, fn, rd=(), wr=(), waits=(), sig=True, chan=None, skip_self=False):
        need = {}
        for b in rd:
            for ev in b.w.values():
                _merge(need, ev)
        for b in wr:
            for ev in b.w.values():
                _merge(need, ev)
            for ev in b.r.values():
                _merge(need, ev)
        for ev in waits:
            _merge(need, ev)
        ws = []
        seen = self.seen[eng]
        selfkey = self.chan[eng].key if (eng in self.chan and self.chan[eng].sem is not None) else None
        for key, ev in need.items():
            if skip_self and key == selfkey:
                continue
            if seen.get(key, 0) >= ev.val:
                continue
            seen[key] = ev.val
            ws.append(ev)
        evo = None
        step = 0
        if sig:
            ch = chan if chan is not None else self.chan[eng]
            evo = ch.next()
            step = ch.step
            for b in rd:
                _merge(b.r, evo)
            for b in wr:
                _merge(b.w, evo)
        self.q[eng].append((fn, ws, evo, step))
        self.nops += 1
        return evo

    def replay(self, eng, e):
        for fn, ws, evo, step in self.q[eng]:
            emb = None
            embed_ok = eng in ("pe", "act", "dve") or (eng == "pool" and getattr(fn, "meth", "dma_start") != "dma_start")
            if embed_ok and ws:
                emb = ws[-1]
                ws = ws[:-1]
            for w in ws:
                e.wait_ge(w.sem, w.val)
            ins = fn(e)
            if emb is not None:
                ins._wait_ge(emb.sem, emb.val)
            if evo is not None:
                ins.then_inc(evo.sem, step)


def layer_blocks():
    B = []

    def win(c0, n):
        return (8, n, [("w_in", 0, c0, n, 0, 0, 8)])

    B.append(win(0, 512))
    B.append(win(512, 256))
    for c in range(4):
        B.append((8, 384, [("w_in", 0, 768 + c * 128, 128, 0, 0, 8),
                           ("w_in", 0, 1280 + c * 128, 128, 0, 128, 8),
                           ("w_in", 0, 1792 + c * 128, 128, 0, 256, 8)]))
    B.append(win(2816, 512))
    B.append(win(2304, 512))
    B.append(win(3328, 512))
    for fp in range(4):
        for br, nm in enumerate(("w_branch_a", "w_branch_b", "w_branch_c")):
            B.append((12, 256, [("w_in", 0, 3840 + br * 1024 + fp * 256, 256, 0, 0, 8),
                                (nm, 0, fp * 256, 256, 8, 0, 4)]))
    for hf in range(2):
        B.append((8, 512, [("w_out", 0, hf * 512, 512, 0, 0, 8)]))
    for hp in range(11):
        B.append((8, 512, [("w_ffn_gate", 0, hp * 256, 256, 0, 0, 8),
                           ("w_ffn_up", 0, hp * 256, 256, 0, 256, 8)]))
    for hf in range(2):
        for (r0, nk) in ((0, 8), (8, 8), (16, 6)):
            B.append((nk, 512, [("w_ffn_down", r0 * 128, hf * 512, 512, 0, 0, nk)]))
    return B


def build(S, L, stop_after=None, dumps=None):
    nc = bass.Bass("TRN2", target_bir_lowering=False)
    P = Prog(nc)
    NT = S // T
    NBK = S // 256
    dumps = dumps if dumps is not None else {}

    def din(name, shape):
        return nc.dram_tensor(name, shape, F32, kind="ExternalInput").ap()

    x_d = din("x", [S, DM])
    W = {
        "w_in": din("w_in", [L, DM, 6912]),
        "w_branch_a": din("w_branch_a", [L, 512, DM]),
        "w_branch_b": din("w_branch_b", [L, 512, DM]),
        "w_branch_c": din("w_branch_c", [L, 512, DM]),
        "w_out": din("w_out", [L, DM, DM]),
        "w_ffn_gate": din("w_ffn_gate", [L, DM, FFN]),
        "w_ffn_up": din("w_ffn_up", [L, DM, FFN]),
        "w_ffn_down": din("w_ffn_down", [L, FFN, DM]),
    }
    sinks_d = din("attn_sinks", [L, 8])
    convw_d = din("conv_w", [L, 3, 512])
    lnp = {n: din(n, [L, DM]) for n in ("ln1_g", "ln1_b", "ln2_g", "ln2_b")}
    out_d = nc.dram_tensor("out", [S, DM], F32, kind="ExternalOutput").ap()

    BLK = layer_blocks()
    NBLK = len(BLK)
    wsc = nc.dram_tensor("wsc", [L, NBLK, 128, 4096], BF16).ap()
    kcache = nc.dram_tensor("kcache", [L, NBK, 2, 96, 1024], BF16).ap()
    vcache = nc.dram_tensor("vcache", [L, NBK, 2, 128, 1024], BF16).ap()
    wsc_bufs = {(l, p_): Buf("wsc%d_%d" % (l, p_)) for l in range(L) for p_ in range(2)}

    def wsc_part(bi):
        return 0 if bi < 21 else 1
    kc_buf = [[Buf("kc%d_%d" % (l, j)) for j in range(NBK)] for l in range(L)]

    sb_off = [(nc.sbuf_base + 63) // 64 * 64]
    sb_top = nc.sbuf_top

    def alloc(name, shape, dtype, at=None):
        nb = int(np.prod(shape[1:])) * (4 if dtype == F32 else 2)
        nb = (nb + 63) // 64 * 64
        if at is None:
            off = sb_off[0]
            sb_off[0] += nb
            assert sb_off[0] <= sb_top, "SBUF overflow at %s: %d > %d" % (name, sb_off[0], sb_top)
        else:
            off = at
        t = nc.alloc_sbuf_tensor_at(name, list(shape), dtype, offset=off)
        b = Buf(name, t)
        b.off = off
        b.nb = nb
        return b

    ident = alloc("ident", [128, 128], BF16)
    tri = alloc("tri", [128, 128], BF16)
    Dq = alloc("Dq", [128, 128], F32)
    btab = alloc("btab", [128, 8, 66], F32)
    pm2 = alloc("pm2", [128, 64], F32)
    swab0 = alloc("swab0", [128, 2, 512], F32)
    swab1 = alloc("swab1", [128, 2, 512], F32)
    esink = alloc("esink", [128, L, 8], F32)
    convw = alloc("convw", [128, L, 4, 3], F32)
    epsb = alloc("epsb", [128, 1], F32)
    ohtab = alloc("ohtab", [128, 32], F32)
    kmf = alloc("kmf", [64, L, 8, 32], F32)
    kmh = alloc("kmh", [64, L, 2, 8, 32], BF16)
    uhalo = alloc("uhalo", [128, L, 4, 2], F32)
    kahalo = alloc("kahalo", [64, L, 2, 128], BF16)
    vahalo = alloc("vahalo", [128, L, 2, 128], BF16)
    xr = [alloc("xr%d" % i, [128, DM], F32) for i in range(4)]
    xbs = [alloc("xb%d" % i, [128, DM], BF16) for i in range(2)]
    xb_n = [0]
    xT = [alloc("xT%d" % i, [128, 8, T], BF16) for i in range(2)]
    lnpar = [alloc("lnpar%d" % i, [128, DM], F32) for i in range(2)]
    lnts = [alloc("lnt%d" % i, [128, DM], F32) for i in range(3)]
    lnt = lnts[0]
    lnst = alloc("lnst", [128, 4, 12], F32)
    lnmv = alloc("lnmv", [128, 4, 2], F32)
    lnrs = alloc("lnrs", [128, 4], F32)
    lnnm = alloc("lnnm", [128, 4], F32)
    ring = [alloc("ring%d" % i, [128, 4096], BF16) for i in range(3)]
    qTa = alloc("qTa", [64, 8, T], BF16)
    kTa = alloc("kTa", [64, 2, 128 + T], BF16)
    va = alloc("va", [128, 5, 2, 128], BF16)
    yaT = alloc("yaT", [128, 4, T], BF16)
    ybT = alloc("ybT", [128, 4, T], BF16)
    ycT = alloc("ycT", [128, 4, T], BF16)
    ubuf = alloc("ubuf", [128, 4, T + 2], F32)
    f32tmp = [alloc("f32tmp%d" % i, [128, T], F32) for i in range(4)]
    pT = [alloc("pT%d" % i, [128, T], BF16) for i in range(4)]
    rec = [alloc("rec%d" % i, [64, T], F32) for i in range(2)]
    gtmp = alloc("gtmp", [128, 8, 32], F32)
    ltmp = alloc("ltmp", [128, 8, 32], F32)
    m8 = alloc("m8", [128, 8, 8], F32)
    mbs = [alloc("mb%d" % i, [128, 8, 32], BF16) for i in range(4)]
    stg = [(alloc("stgk%d" % i, [96, 4, 256], BF16), alloc("stgv%d" % i, [128, 2, 4, 128], BF16)) for i in range(3)]
    u0 = sb_off[0]
    qTg = alloc("qTg", [96, 8, T], BF16)
    kTg = alloc("kTg", [96, 8, T], BF16)
    vg = alloc("vg", [128, 4, 8, 128], BF16)
    u1 = sb_off[0]
    assert u1 - u0 >= NHC * T * 2
    hT = alloc("hT", [128, NHC, T], BF16, at=u0)
    hT_al = [qTg, kTg, vg]
    mergedT = alloc("mergedT", [128, 8, T], BF16)
    sg = [alloc("sg%d" % i, [128, T], F32) for i in range(6)]
    print("SBUF used %d / %d" % (sb_off[0], sb_top))

    banks_t = [nc.alloc_psum_tensor("bank%d" % i, [128, 512], F32) for i in range(8)]
    banks = [Buf("bank%d" % i, banks_t[i]) for i in range(8)]
    bank_rr = [0]

    reserved = set()

    def bank():
        while True:
            i = bank_rr[0] % 8
            bank_rr[0] += 1
            if i not in reserved:
                return banks[i]

    def reserve(n):
        out = []
        for _ in range(n):
            b = bank()
            reserved.add(banks.index(b))
            out.append(b)
        return out

    def unreserve(bs):
        for b in bs:
            reserved.discard(banks.index(b))

    pre_chans = {(l, p_): Chan(P, "pre%d_%d" % (l, p_), 16) for l in range(L) for p_ in range(2)}
    ring_chan = [Chan(P, "ring%d" % i, 16) for i in range(3)]
    stg_chan = [Chan(P, "stg%d" % i, 16) for i in range(3)]
    ld_chan = Chan(P, "ld", 16)
    xld_chan = [Chan(P, "xld%d" % i, 16) for i in range(4)]
    par_chan = [Chan(P, "par%d" % i, 16) for i in range(2)]
    st_chans = [Chan(P, "st%d" % i, 16) for i in range(8)]
    st_rr = [0]

    def dma(eng, out_ap, in_ap, rd, wr, chan, waits=()):
        ws = list(waits)
        if chan.last is not None:
            ws.append(chan.last)
        return P.op(eng, I("dma_start", out=out_ap, in_=in_ap), rd=rd, wr=wr, waits=ws, chan=chan)

    def store(eng, out_ap, in_ap, rd, wr):
        ch = st_chans[st_rr[0] % len(st_chans)]
        st_rr[0] += 1
        return dma(eng, out_ap, in_ap, rd, wr, ch)

    dump_specs = []

    def dump(name, buf, ap, shape, dtype):
        if name not in dumps:
            return
        d = nc.dram_tensor("dbg_" + name, list(shape), dtype, kind="ExternalOutput").ap()
        store("pool", d, ap, [buf], [])
        dump_specs.append(name)

    def pool(fn, rd=(), wr=(), waits=()):
        return P.op("pool", fn, rd=rd, wr=wr, waits=waits)

    def dve(fn, rd=(), wr=(), waits=()):
        return P.op("dve", fn, rd=rd, wr=wr, waits=waits)

    def act(fn, rd=(), wr=(), waits=()):
        return P.op("act", fn, rd=rd, wr=wr, waits=waits)

    pool(I("iota", Dq[:], pattern=[[1, 128]], base=0, channel_multiplier=-1,
                          allow_small_or_imprecise_dtypes=True), wr=[Dq])
    pool(I("tensor_single_scalar", out=ident[:], in_=Dq[:], scalar=0.0, op=ALU.is_equal), rd=[Dq], wr=[ident])
    pool(I("tensor_scalar", out=tri[:], in0=Dq[:], scalar1=0.0, scalar2=NEGB, op0=ALU.is_lt, op1=ALU.mult),
         rd=[Dq], wr=[tri])
    pool(I("iota", lnt[:, 0:66], pattern=[[-128, 66]], base=384, channel_multiplier=1,
                          allow_small_or_imprecise_dtypes=True), wr=[lnt])
    for h in range(8):
        pool(I("tensor_scalar", out=btab[:, h, :], in0=lnt[:, 0:66], scalar1=SLOPES[8 + h], scalar2=None,
                                            op0=ALU.mult), rd=[lnt], wr=[btab])
    pool(I("iota", pm2[:], pattern=[[1, 64]], base=-32, channel_multiplier=0,
           allow_small_or_imprecise_dtypes=True), wr=[pm2])
    pool(I("tensor_scalar", out=pm2[:], in0=pm2[:], scalar1=0.0, scalar2=NEGF, op0=ALU.is_ge, op1=ALU.mult),
         rd=[pm2], wr=[pm2])
    for g in range(2):
        for hl in range(4):
            sl = SLOPES[g * 4 + hl]
            cs = slice(hl * 128, (hl + 1) * 128)
            pool(I("tensor_scalar", out=swab1[:, g, cs], in0=Dq[:], scalar1=0.0, scalar2=NEGF,
                                                        op0=ALU.is_lt, op1=ALU.mult), rd=[Dq], wr=[swab1])
            pool(I("tensor_scalar", out=lnt[:, 128:256], in0=Dq[:], scalar1=-sl, scalar2=None, op0=ALU.mult),
                 rd=[Dq], wr=[lnt])
            pool(I("tensor_tensor", out=swab1[:, g, cs], in0=swab1[:, g, cs], in1=lnt[:, 128:256],
                                                        op=ALU.add), rd=[lnt, swab1], wr=[swab1])
            dve(I("tensor_scalar", out=swab0[:, g, cs], in0=Dq[:], scalar1=0.0, scalar2=NEGF,
                  op0=ALU.is_ge, op1=ALU.mult), rd=[Dq], wr=[swab0])
            dve(I("tensor_scalar", out=lnts[1][:, 256:384], in0=Dq[:], scalar1=128.0, scalar2=-sl,
                  op0=ALU.add, op1=ALU.mult), rd=[Dq], wr=[lnts[1]])
            dve(I("tensor_tensor", out=swab0[:, g, cs], in0=swab0[:, g, cs], in1=lnts[1][:, 256:384],
                  op=ALU.add), rd=[lnts[1], swab0], wr=[swab0])
    pool(I("memset", epsb[:], LN_EPS), wr=[epsb])
    pool(I("memset", ohtab[:], 0.0), wr=[ohtab])
    pool(I("iota", ohtab[64:96, :], pattern=[[1, 32]], base=0, channel_multiplier=-1,
           allow_small_or_imprecise_dtypes=True), wr=[ohtab])
    pool(I("tensor_single_scalar", out=ohtab[64:96, :], in_=ohtab[64:96, :], scalar=0.0, op=ALU.is_equal),
         rd=[ohtab], wr=[ohtab])
    pool(I("memset", kTg[:], 0.0), wr=[kTg])
    pool(I("memset", kmh[:], 0.0), wr=[kmh])
    pool(I("memset", kmf[:], 0.0), wr=[kmf])
    pool(I("memset", uhalo[:], 0.0), wr=[uhalo])
    pool(I("memset", kahalo[:], 0.0), wr=[kahalo])
    pool(I("memset", vahalo[:], 0.0), wr=[vahalo])
    pool(I("memset", va[:], 1.0), wr=[va])
    pool(I("memset", vg[:], 1.0), wr=[vg])
    pool(I("memset", qTg[:], 0.0), wr=[qTg])
    dma("sp", esink[:].rearrange("p l h -> p (l h)"), sinks_d.rearrange("l h -> (l h)").partition_broadcast(128),
        [], [esink], ld_chan)
    act(I("activation", out=esink[:], in_=esink[:], func=AF.Exp), rd=[esink], wr=[esink])
    for l in range(L):
        for c in range(4):
            for k in range(3):
                dma("sp", convw[:, l, c, k:k + 1], convw_d[l, k, c * 128:(c + 1) * 128].rearrange("(p o) -> p o", o=1),
                    [], [convw], ld_chan)

    for s4 in range(4):
        dma("pool", xr[s4][:], x_d[s4 * 128:(s4 + 1) * 128, :], [], [xr[s4]], xld_chan[s4])
    def emit_prepass(l):
        for bi, (kc, ncols, parts) in enumerate(BLK):
            dst = wsc[l, bi, :, 0:kc * ncols].rearrange("p (k n) -> p k n", k=kc)
            for (src, r0, c0, n, dk, dc, nk) in parts:
                s_ap = W[src][l, r0:r0 + nk * 128, c0:c0 + n].rearrange("(k p) n -> p k n", p=128)
                d_ap = dst[:, dk:dk + nk, dc:dc + n]
                pre_ev = P.op("pool", I("dma_start", out=d_ap, in_=s_ap), chan=pre_chans[(l, wsc_part(bi))])
                _merge(wsc_bufs[(l, wsc_part(bi))].w, pre_ev)

    emit_prepass(0)

    class WB:
        pass

    ring_free = [True, True, True]
    seq = [(t, l, bi) for t in range(NT) for l in range(L) for bi in range(NBLK)]
    seq_pos = [0]
    pending = []

    def issue_one():
        free = [i for i in range(3) if ring_free[i]]
        if not free or seq_pos[0] >= len(seq):
            return False
        slot = free[0]
        ring_free[slot] = False
        _, l, bi = seq[seq_pos[0]]
        seq_pos[0] += 1
        kc, ncols, _ = BLK[bi]
        rb = ring[slot]
        dma("sp", rb[:, 0:kc * ncols], wsc[l, bi, :, 0:kc * ncols], [wsc_bufs[(l, wsc_part(bi))]], [rb], ring_chan[slot])
        w = WB()
        w.slot = slot
        w.buf = rb
        w.v = rb[:, 0:kc * ncols].rearrange("p (k n) -> p k n", k=kc)
        pending.append((l, bi, w))
        return True

    def next_block(l, bi):
        while not pending:
            assert issue_one()
        ll, bb, w = pending.pop(0)
        assert (ll, bb) == (l, bi), ((ll, bb), (l, bi))
        return w

    def release(w):
        ring_free[w.slot] = True
        while issue_one():
            pass

    def mm_group(mms, rd, wr):
        n = len(mms)
        ev = None
        for i, fn in enumerate(mms):
            ev = P.op("pe", fn, rd=(rd if i == 0 else ()), wr=(wr if i == 0 else ()), sig=(i == n - 1), skip_self=True)
        for b in rd:
            _merge(b.r, ev)
        for b in wr:
            _merge(b.w, ev)
        return ev

    def proj_fm(w, c0, ncol, xt, pb, prow=None):
        o = pb[0:ncol, :]
        mms = [(I("matmul", o, lhsT=w.v[:, k, c0:c0 + ncol], rhs=xt[:, k, :], start=(k == 0), stop=(k == 7)))
               for k in range(8)]
        return mm_group(mms, [w.buf, xt], [pb])

    def transpose_in(l, xtb):
        for s in range(4):
            xb = xbs[xb_n[0] % 2]
            xb_n[0] += 1
            act(I("activation", out=xb[:], in_=xr[s][:], func=AF.Copy), rd=[xr[s]], wr=[xb])
            pb = bank()
            pbv = pb[:].bitcast(BF16)
            mms = [(I("transpose", out=pbv[:, k * 128:(k + 1) * 128], in_=xb[:, k * 128:(k + 1) * 128],
                      identity=ident[:])) for k in range(8)]
            mm_group(mms, [xb, ident], [pb])
            dve(I("tensor_copy", out=xtb[:, :, s * 128:(s + 1) * 128],
                  in_=pbv.rearrange("p (k t) -> p k t", k=8)), rd=[pb], wr=[xtb])

    def ln_load_params(l, gname, bname):
        dma("sp", lnpar[0][:], lnp[gname][l].partition_broadcast(128), [], [lnpar[0]], par_chan[0])
        dma("sp", lnpar[1][:], lnp[bname][l].partition_broadcast(128), [], [lnpar[1]], par_chan[1])

    def ln_stats_half(s, hf):
        dve(I("bn_stats", out=lnst[:, s, hf * 6:hf * 6 + 6], in_=xr[s][:, hf * 512:(hf + 1) * 512]), rd=[xr[s]], wr=[lnst])

    def ln_aggr(s):
        dve(I("bn_aggr", out=lnmv[:, s, :], in_=lnst[:, s, :]), rd=[lnst], wr=[lnmv])

    def ln_stats(s):
        ln_stats_half(s, 0)
        ln_stats_half(s, 1)
        ln_aggr(s)

    def ln_finish(next_xt):
        act(I("activation", out=lnrs[:], in_=lnmv[:, :, 1], func=AF.Sqrt, bias=epsb[:], scale=1.0),
            rd=[lnmv, epsb], wr=[lnrs])
        dve(I("reciprocal", out=lnrs[:], in_=lnrs[:]), rd=[lnrs], wr=[lnrs])
        dve(I("scalar_tensor_tensor", out=lnnm[:], in0=lnmv[:, :, 0], scalar=-1.0, in1=lnrs[:], op0=ALU.mult, op1=ALU.mult),
            rd=[lnmv, lnrs], wr=[lnnm])
        pend_ev = None
        for s in range(4):
            lt = lnts[s % 3]
            act(I("activation", out=lt[:], in_=xr[s][:], func=AF.Identity, bias=lnnm[:, s:s + 1],
                  scale=lnrs[:, s:s + 1]), rd=[xr[s], lnnm, lnrs], wr=[lt])
            dve(I("tensor_tensor", out=lt[:], in0=lt[:], in1=lnpar[0][:], op=ALU.mult), rd=[lt, lnpar[0]], wr=[lt])
            if next_xt is not None:
                xb = xbs[xb_n[0] % 2]
                xb_n[0] += 1
                dve(I("tensor_tensor", out=xb[:], in0=lt[:], in1=lnpar[1][:], op=ALU.add), rd=[lt, lnpar[1]], wr=[xb])
            pool(I("tensor_tensor", out=xr[s][:], in0=lt[:], in1=lnpar[1][:], op=ALU.add),
                 rd=[lt, lnpar[1]], wr=[xr[s]])
            if next_xt is not None:
                pb = bank()
                pbv = pb[:].bitcast(BF16)
                mms = [(I("transpose", out=pbv[:, k * 128:(k + 1) * 128], in_=xb[:, k * 128:(k + 1) * 128],
                          identity=ident[:])) for k in range(8)]
                mm_group(mms, [xb, ident], [pb])
                if pend_ev is not None:
                    pend_ev()
                pend_ev = (lambda pb=pb, pbv=pbv, s=s: act(I("activation", out=next_xt[:, :, s * 128:(s + 1) * 128],
                           in_=pbv.rearrange("p (k t) -> p k t", k=8), func=AF.Copy), rd=[pb], wr=[next_xt]))
        if pend_ev is not None:
            pend_ev()

    def tile_layer(t, l, xtb, xtb2):
        c0b = 2 * t
        stage = [0]

        def done(name):
            return stop_after is not None and stop_after == (t, l, name)

        if done("init"):
            return False
        LA = 2
        pti = [0]
        units = [(hh_, j_) for hh_ in range(2) for j_ in range(c0b)]
        unit_slot = {}
        next_unit = [0]

        def load_unit():
            if next_unit[0] >= len(units):
                return
            hh_, j_ = units[next_unit[0]]
            next_unit[0] += 1
            slot = stage[0] % 3
            stage[0] += 1
            sk, sv = stg[slot]
            dma("sp", sk[:].rearrange("p h k -> p (h k)"), kcache[l, j_, hh_], [kc_buf[l][j_]], [sk], stg_chan[slot])
            dma("sp", sv[:].rearrange("p c h d -> p (c h d)"), vcache[l, j_, hh_], [kc_buf[l][j_]], [sv], stg_chan[slot])
            unit_slot[(hh_, j_)] = (sk, sv)

        for _ in range(3):
            load_unit()
        w = next_block(l, 0)
        for hp in range(4):
            pb = bank()
            proj_fm(w, hp * 128, 128, xtb, pb)
            act(I("activation", out=qTa[:, 2 * hp, :], in_=pb[0:64, :], func=AF.Copy), rd=[pb], wr=[qTa])
            act(I("activation", out=qTa[:, 2 * hp + 1, :], in_=pb[64:128, :], func=AF.Copy), rd=[pb], wr=[qTa])
        release(w)
        w = next_block(l, 1)
        pool(I("tensor_copy", out=kTa[:, :, 0:128], in_=kahalo[:, l, :, :]), rd=[kahalo], wr=[kTa])
        pool(I("tensor_copy", out=va[:, 0, :, 0:64], in_=vahalo[:, l, :, 0:64]), rd=[vahalo], wr=[va])
        pb = bank()
        proj_fm(w, 0, 128, xtb, pb)
        for g in range(2):
            act(I("activation", out=kTa[:, g, 128:128 + T], in_=pb[g * 64:(g + 1) * 64, :], func=AF.Copy), rd=[pb], wr=[kTa])
        pb = bank()
        for s in range(4):
            o = pb[:, s * 128:(s + 1) * 128]
            mms = [(I("matmul", o, lhsT=xtb[:, k, s * 128:(s + 1) * 128], rhs=w.v[:, k, 128:256],
                                                       start=(k == 0), stop=(k == 7))) for k in range(8)]
            mm_group(mms, [w.buf, xtb], [pb])
        dve(I("tensor_copy", out=va[:, 1:5, :, 0:64],
                                           in_=pb[:, :].rearrange("p (s g d) -> p s g d", s=4, g=2)), rd=[pb], wr=[va])
        pool(I("tensor_copy", out=kahalo[:, l, :, :], in_=kTa[:, :, T:T + 128]), rd=[kTa], wr=[kahalo])
        pool(I("tensor_copy", out=vahalo[:, l, :, 0:64], in_=va[:, 4, :, 0:64]), rd=[va], wr=[vahalo])
        release(w)
        if done("proj"):
            return False
        if "qTa" in dumps and (t, l) == dumps["qTa"]:
            dump("qTa", qTa, qTa[:], [64, 8, T], BF16)
            dump("kTa", kTa, kTa[:], [64, 2, 128 + T], BF16)
            dump("va", va, va[:], [128, 5, 2, 128], BF16)

        def fill_gen():
            dve(I("tensor_copy", out=ubuf[:, :, 0:2], in_=uhalo[:, l, :, :]), rd=[uhalo], wr=[ubuf])
            for c in range(4):
                w = next_block(l, 2 + c)
                ph = bank()
                proj_fm(w, 0, 128, xtb, ph)
                hs = f32tmp[0]
                act(I("activation", out=hs[:], in_=ph[:, :], func=AF.Copy), rd=[ph], wr=[hs])
                pc_ = bank()
                proj_fm(w, 256, 128, xtb, pc_)
                dve(I("tensor_tensor", out=ubuf[:, c, 2:T + 2], in0=pc_[:, :], in1=hs[:], op=ALU.mult),
                    rd=[pc_, hs], wr=[ubuf])
                acc = f32tmp[1]
                dve(I("tensor_scalar", out=acc[:], in0=ubuf[:, c, 0:T], scalar1=convw[:, l, c, 0:1],
                                                            scalar2=None, op0=ALU.mult), rd=[ubuf, convw], wr=[acc])
                dve(I("scalar_tensor_tensor", out=acc[:], in0=ubuf[:, c, 1:T + 1], scalar=convw[:, l, c, 1:2],
                                                                   in1=acc[:], op0=ALU.mult, op1=ALU.add), rd=[ubuf, convw, acc], wr=[acc])
                dve(I("scalar_tensor_tensor", out=acc[:], in0=ubuf[:, c, 2:T + 2], scalar=convw[:, l, c, 2:3],
                                                                   in1=acc[:], op0=ALU.mult, op1=ALU.add), rd=[ubuf, convw, acc], wr=[acc])
                pB = bank()
                proj_fm(w, 128, 128, xtb, pB)
                release(w)
                dve(I("tensor_tensor", out=ybT[:, c, :], in0=pB[:, :], in1=acc[:], op=ALU.mult),
                    rd=[pB, acc], wr=[ybT])
                yield
            dve(I("tensor_copy", out=uhalo[:, l, :, :], in_=ubuf[:, :, T:T + 2]), rd=[ubuf], wr=[uhalo])

            w = next_block(l, 6)
            for hp in range(4):
                pb = bank()
                proj_fm(w, hp * 128, 128, xtb, pb)
                act(I("activation", out=kTg[0:64, 2 * hp, :], in_=pb[0:64, :], func=AF.Copy), rd=[pb], wr=[kTg])
                eva = act(I("activation", out=kTg[0:64, 2 * hp + 1, :], in_=pb[64:128, :], func=AF.Copy), rd=[pb], wr=[kTg])
                for a2 in range(2):
                    dve(I("tensor_reduce", out=kmf[:, l, 2 * hp + a2, c0b:c0b + 2],
                          in_=pb[a2 * 64:(a2 + 1) * 64, :].rearrange("p (b k) -> p b k", b=2),
                          axis=AX.X, op=ALU.add), rd=[pb], wr=[kmf], waits=[eva])
                if hp == 1:
                    yield
            release(w)
            dve(I("tensor_scalar", out=kmf[:, l, :, c0b:c0b + 2], in0=kmf[:, l, :, c0b:c0b + 2], scalar1=1.0 / 256.0,
                                          scalar2=None, op0=ALU.mult), rd=[kmf], wr=[kmf])
            dve(I("tensor_copy", out=kmh[:, l, 0, :, c0b:c0b + 2], in_=kmf[:, l, :, c0b:c0b + 2]), rd=[kmf], wr=[kmh])
            dve(I("tensor_tensor", out=kmh[:, l, 1, :, c0b:c0b + 2], in0=kmf[:, l, :, c0b:c0b + 2],
                                          in1=kmh[:, l, 0, :, c0b:c0b + 2], op=ALU.subtract), rd=[kmf, kmh], wr=[kmh])
            for blk in range(2):
                dve(I("tensor_scalar", out=kTg[64:96, :, blk * 256:(blk + 1) * 256].rearrange("p (c a) k -> p c a k", a=2),
                      in0=ubuf[64:96, :, 0:512].rearrange("p c (a k) -> p c a k", a=2),
                      scalar1=0.0, scalar2=ohtab[64:96, c0b + blk:c0b + blk + 1], op0=ALU.mult, op1=ALU.add),
                    rd=[ubuf, ohtab], wr=[kTg])
            w = next_block(l, 7)
            for hp in range(4):
                pb = bank()
                proj_fm(w, hp * 128, 128, xtb, pb)
                act(I("activation", out=qTg[0:64, 2 * hp, :], in_=pb[0:64, :], func=AF.Copy), rd=[pb], wr=[qTg])
                act(I("activation", out=qTg[0:64, 2 * hp + 1, :], in_=pb[64:128, :], func=AF.Copy), rd=[pb], wr=[qTg])
                if hp == 1:
                    yield
            release(w)
            yield
            pgs = []
            for s in range(4):
                pg = bank()
                mms = []
                for h in range(8):
                    for hi in range(2):
                        mms.append(I("matmul", pg[:, h * 32:(h + 1) * 32], lhsT=qTg[0:64, h, s * 128:(s + 1) * 128],
                                     rhs=kmh[:, l, hi, h, :], start=(hi == 0), stop=(hi == 1)))
                mm_group(mms, [qTg, kmh], [pg])
                pgs.append(pg)
            for s in range(4):
                c = c0b + (s // 2)
                pg = pgs[s]
                dve(I("tensor_tensor", out=gtmp[:], in0=pg[:, 0:256].rearrange("p (h n) -> p h n", h=8),
                      in1=pm2[:, 32 - c:64 - c].rearrange("p (o n) -> p o n", o=1).to_broadcast([128, 8, 32]), op=ALU.add),
                    rd=[pg, pm2], wr=[gtmp])
                for h in range(8):
                    dve(I("max", out=m8[:, h, :], in_=gtmp[:, h, :]), rd=[gtmp], wr=[m8])
                dve(I("tensor_tensor", out=ltmp[:], in0=gtmp[:], in1=m8[:, :, 2:3].to_broadcast([128, 8, 32]), op=ALU.is_lt),
                    rd=[gtmp, m8], wr=[ltmp])
                dve(I("tensor_scalar", out=mbs[s][:], in0=ltmp[:], scalar1=NEGB, scalar2=None, op0=ALU.mult), rd=[ltmp], wr=[mbs[s]])
            pool(I("memset", vg[:, :, :, 64:128], 1.0), wr=[vg])
            w = next_block(l, 8)
            for s in range(4):
                pb = bank()
                o = pb[:, :]
                mms = [(I("matmul", o, lhsT=xtb[:, k, s * 128:(s + 1) * 128], rhs=w.v[:, k, :],
                                                           start=(k == 0), stop=(k == 7))) for k in range(8)]
                mm_group(mms, [w.buf, xtb], [pb])
                dve(I("tensor_copy", out=vg[:, s, :, 0:64], in_=pb[:, :].rearrange("p (h d) -> p h d", h=8)),
                    rd=[pb], wr=[vg])
            release(w)
            for s in range(4):
                pt_ = bank()
                ptv = pt_[:].bitcast(BF16)
                mms = [(I("transpose", out=ptv[64:96, h * 128:(h + 1) * 128], in_=mbs[s][:, h, :], identity=ident[:]))
                       for h in range(8)]
                mm_group(mms, [mbs[s], ident], [pt_])
                dve(I("tensor_copy", out=qTg[64:96, :, s * 128:(s + 1) * 128],
                      in_=ptv[64:96, :].rearrange("p (h q) -> p h q", h=8)), rd=[pt_], wr=[qTg])
            if t < NT - 1:
                for blk in range(2):
                    for hh in range(2):
                        store("pool", kcache[l, c0b + blk, hh].rearrange("p (h k) -> p h k", h=4),
                              kTg[:, hh * 4:(hh + 1) * 4, blk * 256:(blk + 1) * 256], [kTg], [kc_buf[l][c0b + blk]])
                        store("pool", vcache[l, c0b + blk, hh].rearrange("p (c h d) -> p c h d", c=2, h=4),
                              vg[:, 2 * blk:2 * blk + 2, hh * 4:(hh + 1) * 4, :], [vg], [kc_buf[l][c0b + blk]])

            yield

        def swa_stage1(b, g):
            gb = 4 * t + b
            chunks = ([0] if gb > 0 else []) + [1]
            pTs = []
            for pc in chunks:
                pb = bank()
                kcols = slice(b * 128 + (0 if pc == 0 else 128), b * 128 + (128 if pc == 0 else 256))
                mms = []
                for hl in range(4):
                    h = g * 4 + hl
                    mms.append(I("matmul", pb[:, hl * 128:(hl + 1) * 128], lhsT=kTa[:, g, kcols],
                                 rhs=qTa[:, h, b * 128:(b + 1) * 128], start=True, stop=True))
                mm_group(mms, [kTa, qTa], [pb])
                ft = f32tmp[(b * 4 + g * 2 + pc) % 4]
                sw_ = swab0 if pc == 0 else swab1
                dve(I("scalar_tensor_tensor", out=ft[:], in0=pb[:, :], scalar=SCALE, in1=sw_[:, g, :],
                      op0=ALU.mult, op1=ALU.add), rd=[pb, sw_], wr=[ft])
                pt = pT[(b * 4 + g * 2 + pc) % 4]
                act(I("activation", out=pt[:], in_=ft[:], func=AF.Exp), rd=[ft], wr=[pt])
                pTs.append((pc, pt))
            return pTs

        def swa_stage2(b, g, pTs):
            po = bank()
            mms = []
            for i, (pc, pt) in enumerate(pTs):
                vblk = b + pc
                mms.append(I("matmul", po[:, :], lhsT=va[:, vblk, g, :], rhs=pt[:], start=(i == 0), stop=(i == len(pTs) - 1)))
            mm_group(mms, [va] + [p_[1] for p_ in pTs], [po])
            rc = rec[(b * 2 + g) % 2]
            for hl in range(4):
                h = g * 4 + hl
                dve(I("tensor_scalar", out=rc[0:64, hl * 128:(hl + 1) * 128], in0=po[64:128, hl * 128:(hl + 1) * 128],
                      scalar1=esink[64:128, l, h:h + 1], scalar2=None, op0=ALU.add), rd=[po, esink], wr=[rc])
            dve(I("reciprocal", out=rc[0:64, :], in_=rc[0:64, :]), rd=[rc], wr=[rc])
            for par in range(2):
                dve(I("tensor_tensor", out=yaT[par * 64:(par + 1) * 64, 2 * g:2 * g + 2, b * 128:(b + 1) * 128],
                      in0=po[0:64, :].rearrange("p (c a q) -> p c a q", c=2, a=2)[:, :, par, :],
                      in1=rc[0:64, :].rearrange("p (c a q) -> p c a q", c=2, a=2)[:, :, par, :],
                      op=ALU.mult), rd=[po, rc], wr=[yaT])

        filler = fill_gen()
        pend = None
        for b in range(4):
            for g in range(2):
                r_ = swa_stage1(b, g)
                next(filler, None)
                if pend is not None:
                    swa_stage2(*pend)
                pend = (b, g, r_)
        next(filler, None)
        swa_stage2(*pend)
        for _ in filler:
            pass
        if "yaT" in dumps and (t, l) == dumps["yaT"]:
            dump("yaT", yaT, yaT[:], [128, 4, T], BF16)
        if done("swa"):
            return False
        if "qTg" in dumps and (t, l) == dumps["qTg"]:
            dump("qTg", qTg, qTg[:], [96, 8, T], BF16)
            dump("kTg", kTg, kTg[:], [96, 8, T], BF16)
            dump("vg", vg, vg[:], [128, 4, 8, 128], BF16)
            dump("gtmp", gtmp, gtmp[:], [128, 8, 32], F32)

        if done("mgate"):
            return False
        if t == 0 and l == 0:
            for l2 in range(1, L):
                emit_prepass(l2)
        for hh in range(2):
            obanks = reserve(4)
            tasks = [("past", j, hl, ch) for j in range(c0b) for hl in range(4) for ch in range(2)]
            tasks += [("own", sub, hl, 0) for hl in range(4) for sub in range(4)]

            def emit_st(task):
                kind, a, hl, ch = task
                h = hh * 4 + hl
                ps_ = bank()
                if kind == "past":
                    sk, sv = unit_slot[(hh, a)]
                    mm_group([I("matmul", ps_[:, :], lhsT=sk[:, hl, ch * 128:(ch + 1) * 128], rhs=qTg[:, h, :], start=True, stop=True)],
                             [sk, qTg], [ps_])
                elif a == 0:
                    mm_group([
                        I("matmul", ps_[:, 0:256], lhsT=kTg[0:64, h, 0:128], rhs=qTg[0:64, h, 0:256], start=True, stop=False),
                        I("matmul", ps_[:, 0:128], lhsT=ident[:], rhs=tri[:], start=False, stop=False),
                        I("matmul", ps_[:, 256:512], lhsT=kTg[:, h, 0:128], rhs=qTg[:, h, 256:512], start=False, stop=True),
                    ], [kTg, qTg, ident, tri], [ps_])
                elif a == 1:
                    mm_group([
                        I("matmul", ps_[:, 128:256], lhsT=kTg[0:64, h, 128:256], rhs=qTg[0:64, h, 128:256], start=True, stop=False),
                        I("matmul", ps_[:, 128:256], lhsT=ident[:], rhs=tri[:], start=False, stop=False),
                        I("matmul", ps_[:, 256:512], lhsT=kTg[:, h, 128:256], rhs=qTg[:, h, 256:512], start=False, stop=True),
                    ], [kTg, qTg, ident, tri], [ps_])
                elif a == 2:
                    mm_group([
                        I("matmul", ps_[:, 256:512], lhsT=kTg[0:64, h, 256:384], rhs=qTg[0:64, h, 256:512], start=True, stop=False),
                        I("matmul", ps_[:, 256:384], lhsT=ident[:], rhs=tri[:], start=False, stop=True),
                    ], [kTg, qTg, ident, tri], [ps_])
                else:
                    mm_group([
                        I("matmul", ps_[:, 384:512], lhsT=kTg[0:64, h, 384:512], rhs=qTg[0:64, h, 384:512], start=True, stop=False),
                        I("matmul", ps_[:, 384:512], lhsT=ident[:], rhs=tri[:], start=False, stop=True),
                    ], [kTg, qTg, ident, tri], [ps_])
                return ps_

            def emit_ep(task, ps_):
                kind, a, hl, ch = task
                h = hh * 4 + hl
                if kind == "past":
                    sk, sv = unit_slot[(hh, a)]
                    cols = slice(0, T)
                    mi = (4 * t - 2 * a - ch) + 3
                    vl, vb = sv[:, ch, hl, :], [sv]
                    first = (a == 0 and ch == 0)
                    last = False
                else:
                    cols = (slice(0, 512), slice(128, 512), slice(256, 512), slice(384, 512))[a]
                    mi = 3 - a
                    vl, vb = vg[:, a, h, :], [vg]
                    first = (c0b == 0 and a == 0)
                    last = (a == 3)
                pt = pT[pti[0] % 4]
                pti[0] += 1
                act(I("activation", out=pt[:, cols], in_=ps_[:, cols], func=AF.Exp, bias=btab[:, h, mi:mi + 1], scale=SCALE),
                    rd=[ps_, btab], wr=[pt])
                ob = obanks[hl]
                mm_group([I("matmul", ob[:, cols], lhsT=vl, rhs=pt[:, cols], start=first, stop=last)], [pt] + vb, [ob])

            inflight = []
            for i in range(len(tasks) + LA):
                if i < len(tasks):
                    inflight.append((tasks[i], emit_st(tasks[i])))
                if i >= LA:
                    task, ps_ = inflight.pop(0)
                    emit_ep(task, ps_)
                    if task[0] == "past" and task[2] == 3 and task[3] == 1:
                        load_unit()
            for hl in range(4):
                h = hh * 4 + hl
                ob = obanks[hl]
                rc = rec[hl % 2]
                dve(I("reciprocal", out=rc[0:64, :], in_=ob[64:128, :]), rd=[ob], wr=[rc])
                dve(I("tensor_tensor", out=ycT[(h % 2) * 64:(h % 2) * 64 + 64, h // 2, :],
                      in0=ob[0:64, :], in1=rc[0:64, :], op=ALU.mult), rd=[ob, rc], wr=[ycT])
            unreserve(obanks)
        if "ycT" in dumps and (t, l) == dumps["ycT"]:
            dump("ycT", ycT, ycT[:], [128, 4, T], BF16)
        if done("moba"):
            return False

        yT = [yaT, ybT, ycT]
        for fp in range(4):
            for br in range(3):
                w = next_block(l, 9 + fp * 3 + br)
                for f2 in range(2):
                    pb = bank()
                    proj_fm(w, f2 * 128, 128, xtb, pb)
                    s_ = sg[f2 * 3 + br]
                    act(I("activation", out=s_[:], in_=pb[:, :], func=AF.Sigmoid), rd=[pb], wr=[s_])
                    pbr = bank()
                    mms = [(I("matmul", pbr[:, :], lhsT=w.v[:, 8 + k, f2 * 128:(f2 + 1) * 128], rhs=yT[br][:, k, :],
                        start=(k == 0), stop=(k == 3))) for k in range(4)]
                    mm_group(mms, [w.buf, yT[br]], [pbr])
                    dve(I("tensor_tensor", out=s_[:], in0=pbr[:, :], in1=s_[:], op=ALU.mult),
                        rd=[pbr, s_], wr=[s_])
                release(w)
            for f2 in range(2):
                fc = fp * 2 + f2
                a_, b_, c_ = sg[f2 * 3], sg[f2 * 3 + 1], sg[f2 * 3 + 2]
                pool(I("tensor_tensor", out=a_[:], in0=a_[:], in1=b_[:], op=ALU.add),
                     rd=[a_, b_], wr=[a_])
                pool(I("tensor_tensor", out=mergedT[:, fc, :], in0=a_[:], in1=c_[:], op=ALU.add),
                     rd=[a_, c_], wr=[mergedT])
        if "mergedT" in dumps and (t, l) == dumps["mergedT"]:
            dump("mergedT", mergedT, mergedT[:], [128, 8, T], BF16)

        if done("merge"):
            return False
        ln_load_params(l, "ln1_g", "ln1_b")
        wo = [next_block(l, 21), next_block(l, 22)]
        for s in range(4):
            for hf in range(2):
                w = wo[hf]
                pb = bank()
                mms = [(I("matmul", pb[:, :], lhsT=mergedT[:, k, s * 128:(s + 1) * 128], rhs=w.v[:, k, :],
                          start=(k == 0), stop=(k == 7))) for k in range(8)]
                mm_group(mms, [w.buf, mergedT], [pb])
                dve(I("scalar_tensor_tensor", out=xr[s][:, hf * 512:(hf + 1) * 512], in0=xr[s][:, hf * 512:(hf + 1) * 512], scalar=ALPHA,
                      in1=pb[:, :], op0=ALU.mult, op1=ALU.add), rd=[pb, xr[s]], wr=[xr[s]])
            ln_stats(s)
        release(wo[0])
        release(wo[1])
        if done("outp"):
            return False
        ln_finish(xtb2)
        if "x1" in dumps and (t, l) == dumps["x1"]:
            dump("x1", xr[0], xr[0][:], [128, DM], F32)
        if done("ln1"):
            return False
        if done("tr2"):
            return False

        for hp in range(11):
            w = next_block(l, 23 + hp)
            for h2 in range(2):
                hc = hp * 2 + h2
                pgt = bank()
                proj_fm(w, h2 * 128, 128, xtb2, pgt)
                s_ = sg[hc % 2]
                act(I("activation", out=s_[:], in_=pgt[:, :], func=AF.Silu), rd=[pgt], wr=[s_])
                pu = bank()
                proj_fm(w, 256 + h2 * 128, 128, xtb2, pu)
                dve(I("tensor_tensor", out=hT[:, hc, :], in0=pu[:, :], in1=s_[:], op=ALU.mult),
                    rd=[pu, s_], wr=hT_al)
            release(w)
        if done("ffn1"):
            return False
        ln_load_params(l, "ln2_g", "ln2_b")
        for hf in range(2):
            pbs = reserve(4)
            k0 = 0
            for bi3, nk in enumerate((8, 8, 6)):
                w = next_block(l, 34 + hf * 3 + bi3)
                for s in range(4):
                    mms = [(I("matmul", pbs[s][:, :], lhsT=hT[:, k0 + k, s * 128:(s + 1) * 128], rhs=w.v[:, k, :],
                                                                 start=(k0 + k == 0), stop=(k0 + k == NHC - 1))) for k in range(nk)]
                    mm_group(mms, [w.buf] + hT_al, [pbs[s]])
                release(w)
                k0 += nk
                if done("dn%d" % bi3):
                    return False
            unreserve(pbs)
            for s in range(4):
                dve(I("scalar_tensor_tensor", out=xr[s][:, hf * 512:(hf + 1) * 512], in0=xr[s][:, hf * 512:(hf + 1) * 512], scalar=ALPHA,
                    in1=pbs[s][:, :], op0=ALU.mult, op1=ALU.add), rd=[pbs[s], xr[s]], wr=[xr[s]])
                ln_stats_half(s, hf)
                if hf == 1:
                    ln_aggr(s)
        if done("ffn2"):
            return False
        ln_finish(xtb if l < L - 1 else None)
        return True

    ok = True
    for t in range(NT):
        for s4 in range(4):
            if t > 0:
                dma("pool", xr[s4][:], x_d[t * T + s4 * 128:t * T + (s4 + 1) * 128, :], [], [xr[s4]], xld_chan[s4])
        for l in range(L):
            xa, xb2 = xT[0], xT[1]
            if l == 0:
                transpose_in(l, xa)
            if "xT" in dumps and (t, l) == dumps["xT"]:
                dump("xT", xa, xa[:], [128, 8, T], BF16)
            ok = tile_layer(t, l, xa, xb2)
            if not ok:
                break
        if not ok:
            break
        for s4 in range(4):
            store("pool", out_d[t * T + s4 * 128:t * T + (s4 + 1) * 128, :], xr[s4][:], [xr[s4]], [])
    fin = [ch.last for ch in st_chans if ch.last is not None]
    P.op("pool", I("engine_nop", ), waits=fin, sig=False)

    with nc.Block() as block:
        @block.tensor
        def _(e):
            P.replay("pe", e)

        @block.scalar
        def _(e):
            P.replay("act", e)

        @block.vector
        def _(e):
            P.replay("dve", e)

        @block.gpsimd
        def _(e):
            P.replay("pool", e)

        @block.sync
        def _(e):
            P.replay("sp", e)
    print("ops:", {k: len(v) for k, v in P.q.items()})
    return nc, dump_specs


_CACHE = {}


def run(inputs, S, L, n_cores, stop_after=None, dumps=None, layer_sel=None):
    key = (S, L, stop_after, str(dumps))
    if key not in _CACHE:
        _CACHE[key] = build(S, L, stop_after=stop_after, dumps=dumps)
    nc, dump_specs = _CACHE[key]
    names = ["w_in", "attn_sinks", "conv_w", "w_branch_a", "w_branch_b", "w_branch_c", "w_out", "ln1_g", "ln1_b",
             "w_ffn_gate", "w_ffn_up", "w_ffn_down", "ln2_g", "ln2_b"]
    shared = {}
    for n in names:
        a = np.asarray(inputs[n], dtype=np.float32)
        if layer_sel is not None:
            a = a[layer_sel:layer_sel + 1]
        shared[n] = np.ascontiguousarray(a)
    x = np.asarray(inputs["x"], dtype=np.float32)
    in_maps = []
    for c in range(n_cores):
        m = dict(shared)
        m["x"] = np.ascontiguousarray(x[c])
        in_maps.append(m)
    res = run_bass_kernel_spmd(nc, in_maps, core_ids=list(range(n_cores)))
    return res, dump_specs


def kernel(**inputs):
    x = np.asarray(inputs["x"])
    B, S, _ = x.shape
    res, _ = run(inputs, S, 2, B)
    return np.stack([np.asarray(r["out"]) for r in res.results], axis=0).astype(np.float32)
```
